# Optimizing a Trainium2 kernel written in Bass

```python
import jax, jax.numpy as jnp
from jax import lax
import numpy as np

D_MODEL = 4096
BATCH = 8
SEQ = 2048
DEPTH = 4

CHUNK = 64
H_A = 8
DK_A = 128
DV_A = 128
W_A = H_A * DK_A
H_B = 8
W_B = 1024
BW_B = W_B // H_B
CONV_K = 4
C_RG = 8.0
D_FF = 4 * D_MODEL
SPLITS = (W_A, 2 * W_A, 3 * W_A, 4 * W_A, 4 * W_A + W_B, 4 * W_A + 2 * W_B,
          4 * W_A + 2 * W_B + D_MODEL)
IN_COLS = 4 * W_A + 2 * W_B + 2 * D_MODEL
ALPHA = (2.0 * DEPTH) ** 0.25
BETA = (8.0 * DEPTH) ** -0.25
EPS = 1e-5
ROPE_BASE = 10000.0

kernel_name = "hybrid_retention_rglru_deepnorm_trunk"


def layer_norm(x, g, b):
    xf = x.astype(jnp.float32)
    mu = jnp.mean(xf, axis=-1, keepdims=True)
    var = jnp.mean(jnp.square(xf - mu), axis=-1, keepdims=True)
    return ((xf - mu) * lax.rsqrt(var + EPS) * g + b).astype(x.dtype)


def rotary(t, cos, sin):
    t1, t2 = jnp.split(t.astype(jnp.float32), 2, axis=-1)
    c = cos[None, :, None, :]
    s = sin[None, :, None, :]
    return jnp.concatenate([t1 * c - t2 * s, t2 * c + t1 * s], axis=-1)


def retention(q, k, v):
    b_, s_, h, dk = q.shape
    dv = v.shape[-1]
    nc = s_ // CHUNK
    lg = jnp.log1p(-jnp.exp2(-5.0 - jnp.arange(h, dtype=jnp.float32)))
    idx = jnp.arange(CHUNK, dtype=jnp.float32)
    intra_decay = jnp.exp(lg[:, None, None] * jnp.abs(idx[:, None] - idx[None, :]))
    q_decay = jnp.exp(lg[:, None] * (idx + 1.0))
    k_decay = jnp.exp(lg[:, None] * (CHUNK - 1.0 - idx))
    chunk_decay = jnp.exp(lg * CHUNK)
    qc = q.astype(jnp.float32).reshape(b_, nc, CHUNK, h, dk)
    kc = k.astype(jnp.float32).reshape(b_, nc, CHUNK, h, dk) * (dk ** -0.5)
    vc = v.astype(jnp.float32).reshape(b_, nc, CHUNK, h, dv)
    scores = jnp.einsum('bnihd,bnjhd->bhnij', qc, kc) * intra_decay[:, None]
    o_intra = jnp.einsum('bhnij,bnjhe->bnihe', scores, vc)
    kv = jnp.einsum('bnjhd,hj,bnjhe->nbhde', kc, k_decay, vc)

    def step(state, kv_c):
        return chunk_decay[None, :, None, None] * state + kv_c, state

    _, s_prev = lax.scan(step, jnp.zeros_like(kv[0]), kv)
    o_inter = jnp.einsum('bnihd,hi,nbhde->bnihe', qc, q_decay, s_prev)
    return (o_intra + o_inter).reshape(b_, s_, h, dv)


def head_group_norm(o, gain):
    mu = jnp.mean(o, axis=-1, keepdims=True)
    var = jnp.mean(jnp.square(o - mu), axis=-1, keepdims=True)
    y = (o - mu) * lax.rsqrt(var + EPS)
    return y.reshape(o.shape[0], o.shape[1], -1) * gain


def causal_conv(x, w, b):
    s_ = x.shape[1]
    xp = jnp.pad(x, ((0, 0), (CONV_K - 1, 0), (0, 0)))
    out = xp[:, 0:s_] * w[0]
    for tap in range(1, CONV_K):
        out = out + xp[:, tap:tap + s_] * w[tap]
    return out + b


def rg_lru(x, w_a, b_a, w_x, b_x, lam):
    b_, s_, wd = x.shape
    xh = x.reshape(b_, s_, H_B, BW_B)
    r = jax.nn.sigmoid(jnp.einsum('bshi,hij->bshj', xh, w_a).reshape(b_, s_, wd) + b_a)
    i = jax.nn.sigmoid(jnp.einsum('bshi,hij->bshj', xh, w_x).reshape(b_, s_, wd) + b_x)
    log_a = -C_RG * r.astype(jnp.float32) * jax.nn.softplus(-lam.astype(jnp.float32))
    a = jnp.exp(log_a)
    u = jnp.sqrt(-jnp.expm1(2.0 * log_a)) * (i * x).astype(jnp.float32)

    def combine(lhs, rhs):
        a1, b1 = lhs
        a2, b2 = rhs
        return a1 * a2, a2 * b1 + b2

    _, h = lax.associative_scan(combine, (a, u), axis=1)
    return h.astype(x.dtype)


def setup_inputs(seed: int = 0) -> dict:
    key = jax.random.key(seed)
    ks = jax.random.split(key, 20)
    f32 = jnp.float32
    nrm = lambda k, shape, scale: jax.random.normal(k, shape, f32) * scale
    x = jax.random.normal(ks[0], (BATCH, SEQ, D_MODEL), f32)
    w_in = nrm(ks[1], (DEPTH, D_MODEL, IN_COLS), D_MODEL ** -0.5)
    ret_gn_w = 1.0 + nrm(ks[2], (DEPTH, W_A), 0.02)
    conv_w = nrm(ks[3], (DEPTH, CONV_K, W_B), CONV_K ** -0.5)
    conv_b = nrm(ks[4], (DEPTH, W_B), 0.01)
    rg_wa = nrm(ks[5], (DEPTH, H_B, BW_B, BW_B), BW_B ** -0.5)
    rg_ba = nrm(ks[6], (DEPTH, W_B), 0.01)
    rg_wx = nrm(ks[7], (DEPTH, H_B, BW_B, BW_B), BW_B ** -0.5)
    rg_bx = nrm(ks[8], (DEPTH, W_B), 0.01)
    u = jax.random.uniform(ks[9], (DEPTH, W_B), f32, 0.9, 0.999)
    p = u ** (1.0 / C_RG)
    rg_lambda = jnp.log(p) - jnp.log1p(-p)
    w_br_a = nrm(ks[10], (DEPTH, W_A, D_MODEL), (W_A ** -0.5) * BETA)
    w_br_b = nrm(ks[11], (DEPTH, W_B, D_MODEL), (W_B ** -0.5) * BETA)
    b_gate = nrm(ks[12], (DEPTH, 2 * D_MODEL), 0.01)
    w_o = nrm(ks[13], (DEPTH, D_MODEL, D_MODEL), (D_MODEL ** -0.5) * BETA)
    ln1_g = 1.0 + nrm(ks[14], (DEPTH, D_MODEL), 0.02)
    ln1_b = nrm(ks[15], (DEPTH, D_MODEL), 0.01)
    w_up = nrm(ks[16], (DEPTH, D_MODEL, D_FF), D_MODEL ** -0.5)
    w_down = nrm(ks[17], (DEPTH, D_FF, D_MODEL), (D_FF ** -0.5) * BETA)
    ln2_g = 1.0 + nrm(ks[18], (DEPTH, D_MODEL), 0.02)
    ln2_b = nrm(ks[19], (DEPTH, D_MODEL), 0.01)
    return {"x": x, "w_in": w_in, "ret_gn_w": ret_gn_w, "conv_w": conv_w,
            "conv_b": conv_b, "rg_wa": rg_wa, "rg_ba": rg_ba, "rg_wx": rg_wx,
            "rg_bx": rg_bx, "rg_lambda": rg_lambda, "w_br_a": w_br_a,
            "w_br_b": w_br_b, "b_gate": b_gate, "w_o": w_o, "ln1_g": ln1_g,
            "ln1_b": ln1_b, "w_up": w_up, "w_down": w_down, "ln2_g": ln2_g,
            "ln2_b": ln2_b}


def reference(x, w_in, ret_gn_w, conv_w, conv_b, rg_wa, rg_ba, rg_wx, rg_bx,
              rg_lambda, w_br_a, w_br_b, b_gate, w_o, ln1_g, ln1_b, w_up, w_down,
              ln2_g, ln2_b):
    b_, s_, _ = x.shape
    pos = jnp.arange(s_, dtype=jnp.float32)
    inv_freq = ROPE_BASE ** (-jnp.arange(0, DK_A, 2, dtype=jnp.float32) / DK_A)
    ang = pos[:, None] * inv_freq[None, :]
    cos, sin = jnp.cos(ang), jnp.sin(ang)

    for l in range(DEPTH):
        proj = x @ w_in[l]
        q, k, v, g_ret, xr, gr, mg_a, mg_b = jnp.split(proj, SPLITS, axis=-1)
        qh = rotary(q.reshape(b_, s_, H_A, DK_A), cos, sin)
        kh = rotary(k.reshape(b_, s_, H_A, DK_A), cos, sin)
        vh = v.reshape(b_, s_, H_A, DV_A)
        o = head_group_norm(retention(qh, kh, vh), ret_gn_w[l]).astype(x.dtype)
        y_a = (jax.nn.silu(g_ret) * o) @ w_br_a[l]
        xc = causal_conv(xr, conv_w[l], conv_b[l])
        hr = rg_lru(xc, rg_wa[l], rg_ba[l], rg_wx[l], rg_bx[l], rg_lambda[l])
        y_b = (hr * jax.nn.gelu(gr)) @ w_br_b[l]
        bg_a, bg_b = jnp.split(b_gate[l], 2)
        merged = jax.nn.sigmoid(mg_a + bg_a) * y_a + jax.nn.sigmoid(mg_b + bg_b) * y_b
        x = layer_norm(ALPHA * x + merged @ w_o[l], ln1_g[l], ln1_b[l])
        ff = jnp.square(jax.nn.relu(x @ w_up[l])) @ w_down[l]
        x = layer_norm(ALPHA * x + ff, ln2_g[l], ln2_b[l])
    return x
```

```python
import numpy as np
import ml_dtypes
import concourse.bass as bass
import concourse.mybir as mybir
from concourse.bass_utils import run_bass_kernel_spmd

F32 = mybir.dt.float32
BF16 = mybir.dt.bfloat16
AF = mybir.ActivationFunctionType
ALU = mybir.AluOpType

CHUNK = 64
EPS = 1e-5
C_RG = 8.0
ROPE_BASE = 10000.0


class Cfg:
    def __init__(self, D=4096, H=8, DFF=16384, T=2048, DEPTH=4):
        self.D, self.H, self.DFF, self.T, self.DEPTH = D, H, DFF, T, DEPTH
        self.DC = D // 128
        self.FC = DFF // 128
        self.TT = T // 512
        self.TB = T // 128
        self.WA = 128 * H
        self.INC = 4 * self.WA + 2 * self.WA + 2 * D
        self.MW = T + 384
        self.ALPHA = (2.0 * DEPTH) ** 0.25
        self.KB = self.FC // self.DC
        need = max(2 * (3 * T * 2 + self.MW * 4 + 128), 7 * (T * 4 + 64) + T * 2 + 64)
        self.NP = max(self.DC, 2 * H + -(-need // (T * 2)))
        assert self.FC % self.DC == 0 and T % 512 == 0 and self.DC % 4 == 0


class Res:
    __slots__ = ("w", "r", "name", "excl")

    def __init__(self, name="", excl=False):
        self.w = {}
        self.r = {}
        self.name = name
        self.excl = excl


class Item:
    __slots__ = ("waits", "fns", "ev", "dma")

    def __init__(self, waits, fns, ev, dma):
        self.waits, self.fns, self.ev, self.dma = waits, fns, ev, dma


class Planner:
    QS = ("pe", "act", "dve", "pool", "sp")

    def __init__(self):
        self.items = {q: [] for q in self.QS}
        self.known = {q: {} for q in self.QS}
        self.dma_cnt = {}
        self.needed = {q: set() for q in self.QS}

    def new_dsem(self, key):
        self.dma_cnt[key] = 0
        return key

    def emit(self, q, fns, reads=(), writes=(), dsem=None):
        if not isinstance(fns, (list, tuple)):
            fns = [fns]
        ex = [r for r in reads if r.excl]
        if ex:
            reads = [r for r in reads if not r.excl]
            writes = list(writes) + ex
        deps = {}
        for r in reads:
            for k, v in r.w.items():
                if deps.get(k, -1) < v:
                    deps[k] = v
        for w in writes:
            for k, v in w.w.items():
                if deps.get(k, -1) < v:
                    deps[k] = v
            for k, v in w.r.items():
                if deps.get(k, -1) < v:
                    deps[k] = v
        kn = self.known[q]
        waits = []
        idx = len(self.items[q])
        for k, v in deps.items():
            if k == q and q == "pe" and dsem is None:
                continue
            if kn.get(k, -1) >= v:
                continue
            kn[k] = v
            waits.append((k, v))
            if k in self.needed:
                self.needed[k].add(v)
        if dsem is not None:
            self.dma_cnt[dsem] += 1
            ev = (dsem, self.dma_cnt[dsem])
        else:
            ev = (q, idx)
        self.items[q].append(Item(waits, fns, ev, dsem is not None))
        for r in reads:
            if r.r.get(ev[0], -1) < ev[1]:
                r.r[ev[0]] = ev[1]
        for w in writes:
            w.w[ev[0]] = ev[1]
        return ev

    def final_wait(self, q, keys):
        waits = []
        for k in keys:
            v = self.dma_cnt[k]
            if v > 0 and self.known[q].get(k, -1) < v:
                waits.append((k, v))
        self.items[q].append(Item(waits, [], None, False))

    def replay(self, nc, block, esems, dsems):
        val = {}
        for q in self.QS:
            nd = sorted(self.needed[q])
            val[q] = {idx: i + 1 for i, idx in enumerate(nd)}

        def run(q, eng):
            vq = val[q]
            for idx, it in enumerate(self.items[q]):
                for (k, v) in it.waits:
                    if k in esems:
                        eng.wait_ge(esems[k], val[k][v])
                    else:
                        eng.wait_ge(dsems[k], 16 * v)
                ins = None
                for f in it.fns:
                    ins = f(eng)
                if ins is None:
                    continue
                if it.dma:
                    ins.then_inc(dsems[it.ev[0]], 16)
                elif idx in vq:
                    ins.then_inc(esems[q], 1)

        @block.tensor
        def _(e):
            run("pe", e)

        @block.scalar
        def _(e):
            run("act", e)

        @block.vector
        def _(e):
            run("dve", e)

        @block.gpsimd
        def _(e):
            run("pool", e)

        @block.sync
        def _(e):
            run("sp", e)


class Slot:
    __slots__ = ("ap", "res", "sem")

    def __init__(self, ap, res, sem):
        self.ap, self.res, self.sem = ap, res, sem


class Ring:
    def __init__(self, slots):
        self.slots = slots
        self.i = 0

    def next(self):
        s = self.slots[self.i % len(self.slots)]
        self.i += 1
        return s


def build_program(cfg):
    D, H, T, DC, FC, TT, TB, WA = cfg.D, cfg.H, cfg.T, cfg.DC, cfg.FC, cfg.TT, cfg.TB, cfg.WA
    DEPTH, INC, MW, ALPHA, KB = cfg.DEPTH, cfg.INC, cfg.MW, cfg.ALPHA, cfg.KB
    import os
    STOP = int(os.environ.get("KSTOP", "-1"))
    nc = bass.Bass("TRN2", target_bir_lowering=False)
    P = Planner()

    def din(name, shape, dt=F32):
        return nc.dram_tensor(name, list(shape), dt, kind="ExternalInput").ap()

    def dscr(name, shape, dt=F32):
        return nc.dram_tensor(name, list(shape), dt, kind="Internal").ap()

    x_in = din("x", [T, D])
    w_in = din("w_in", [DEPTH, D, INC])
    w_bra = din("w_br_a", [DEPTH, WA, D])
    w_brb = din("w_br_b", [DEPTH, WA, D])
    w_o = din("w_o", [DEPTH, D, D])
    w_up = din("w_up", [DEPTH, D, cfg.DFF])
    w_dn = din("w_down", [DEPTH, cfg.DFF, D])
    rg_wa = din("rg_wa", [DEPTH, H, 128, 128])
    rg_wx = din("rg_wx", [DEPTH, H, 128, 128])
    NV = 4 * DC + 2 * DC + 10 * H
    vecs = din("vecs", [DEPTH, 128, NV])
    cos_t = din("cos_t", [128, T])
    sin_t = din("sin_t", [128, T])
    mask_t = din("mask_t", [H, 128, MW])
    ident_in = din("ident", [128, 128])
    y_out = nc.dram_tensor("y", [T, D], F32, kind="ExternalOutput").ap()

    S_xres = dscr("S_xres", [DC, 128, T])
    S_z = dscr("S_z", [DC, 128, T])
    S_q = dscr("S_q", [H, 128, T], BF16)
    S_k = dscr("S_k", [H, 128, T], BF16)
    S_v = dscr("S_v", [T, WA], BF16)
    S_g = dscr("S_g", [H, 128, T])
    S_xr = dscr("S_xr", [H, 128, T])
    S_gr = dscr("S_gr", [H, 128, T])
    S_sga = dscr("S_sga", [DC, 128, T])
    S_sgb = dscr("S_sgb", [DC, 128, T])
    S_mg = dscr("S_mg", [DC, 128, T], BF16)
    S_h = dscr("S_h", [FC, 128, T], BF16)

    def grid(name, n1, n2):
        return [[Res(f"{name}{i}_{j}") for j in range(n2)] for i in range(n1)]

    R_xres, R_z = grid("xres", DC, TT), grid("z", DC, TT)
    R_q, R_k, R_g, R_xr, R_gr = (grid(n, H, TT) for n in ("q", "k", "g", "xr", "gr"))
    R_v = grid("v", TB, WA // 256)
    R_sga, R_sgb, R_mg = grid("sga", DC, TT), grid("sgb", DC, TT), grid("mg", DC, TT)
    R_h = grid("h", FC, TT)

    import contextlib
    es = contextlib.ExitStack()
    with es:
        def sb(name, shape, dt=F32):
            return es.enter_context(nc.sbuf_tensor(name, list(shape), dt))

        NP = cfg.NP
        arena = sb("arena", [128, NP, T], BF16)
        R_page = [Res(f"page{i}") for i in range(NP)]
        WSL = 8192
        wslots = [Slot(sb(f"wslot{i}", [128, WSL], BF16), Res(f"wslot{i}"), P.new_dsem(f"w{i}")) for i in range(2)]
        wring = Ring(wslots)
        NLD, NEV, NBF = 5, 5, 4
        ldring = Ring([Slot(sb(f"ld{i}", [128, 512]), Res(f"ld{i}"), P.new_dsem(f"ld{i}")) for i in range(NLD)])
        evring = Ring([Slot(sb(f"ev{i}", [128, 512]), Res(f"ev{i}"), P.new_dsem(f"ev{i}")) for i in range(NEV)])
        bfring = Ring([Slot(sb(f"bf{i}", [128, 512], BF16), Res(f"bf{i}"), P.new_dsem(f"bf{i}")) for i in range(NBF)])
        accS = [Slot(sb(f"accS{i}", [128, 512]), Res(f"accS{i}"), None) for i in range(TT)]
        accQ = [Slot(sb(f"accQ{i}", [128, 512]), Res(f"accQ{i}"), None) for i in range(TT)]
        ident = sb("ident_sb", [128, 128]); R_ident = Res("ident"); S_ident = P.new_dsem("ident")
        onesM = sb("onesM", [128, 128]); R_ones = Res("ones")
        onesD = sb("onesD", [128, 128]); R_onesD = Res("onesD")
        vec_sb = sb("vec_sb", [128, NV]); R_vec = Res("vec"); S_vec = P.new_dsem("vec")
        gw_sb = sb("gw_sb", [128, 2 * H, 128], BF16); R_gw = Res("gw"); S_gw = P.new_dsem("gw")
        sc_sb = sb("sc_sb", [128, H]); R_sc = Res("sc")
        sct_sb = sb("sct_sb", [128, H]); R_sct = Res("sct")
        NPS = 8
        psb = [Slot(es.enter_context(nc.psum_tensor(f"ps{i}", [128, 512], F32)), Res(f"ps{i}", excl=True), None) for i in range(NPS)]
        psring = Ring(psb)
        asem = [P.new_dsem(f"a{i}") for i in range(4)]
        gsem = [P.new_dsem(f"g{i}") for i in range(8)]
        gring = Ring(gsem)
        osem = P.new_dsem("out")

        V_LN1G, V_LN1B, V_LN2G, V_LN2B = 0, DC, 2 * DC, 3 * DC
        V_BGA, V_BGB = 4 * DC, 5 * DC
        V_GN = 6 * DC
        V_CW = V_GN + H
        V_CB = V_CW + 4 * H
        V_BA = V_CB + H
        V_BX = V_BA + H
        V_LAM = V_BX + H

        def vcol(off, i):
            return vec_sb[:, off + i:off + i + 1]

        def cols(tt):
            return slice(tt * 512, (tt + 1) * 512)

        def dma(q, out, in_, reads, writes, sem):
            return P.emit(q, lambda e: e.dma_start(out=out, in_=in_), reads=reads, writes=writes, dsem=sem)

        def load_tile(ring, src_ap, src_res, q="sp"):
            s = ring.next()
            dma(q, s.ap[:], src_ap, [src_res] if src_res is not None else [], [s.res], s.sem)
            return s

        def store_tile(s, dst_ap, dst_res, q="sp", ap=None):
            dma(q, dst_ap, s.ap[:] if ap is None else ap, [s.res], [dst_res], s.sem)

        def act(out, in_, func, reads, writes, bias=None, scale=None):
            kw = {}
            if bias is not None:
                kw["bias"] = bias
            if scale is not None:
                kw["scale"] = scale
            return P.emit("act", lambda e: e.activation(out=out, in_=in_, func=func, **kw), reads=reads, writes=writes)

        def tt_op(out, in0, in1, op, reads, writes, q="dve"):
            return P.emit(q, lambda e: e.tensor_tensor(out=out, in0=in0, in1=in1, op=op), reads=reads, writes=writes)

        def ts_op(out, in0, s1, s2, op0, op1, reads, writes, q="dve"):
            if op1 is None:
                return P.emit(q, lambda e: e.tensor_scalar(out=out, in0=in0, scalar1=s1, scalar2=None, op0=op0), reads=reads, writes=writes)
            return P.emit(q, lambda e: e.tensor_scalar(out=out, in0=in0, scalar1=s1, scalar2=s2, op0=op0, op1=op1), reads=reads, writes=writes)

        def stt_op(out, in0, scalar, in1, op0, op1, reads, writes):
            return P.emit("dve", lambda e: e.scalar_tensor_tensor(out=out, in0=in0, scalar=scalar, in1=in1, op0=op0, op1=op1), reads=reads, writes=writes)

        def mm_group(ps, pairs, reads, q_writes):
            n = len(pairs)
            fns = []
            for i, (l, r) in enumerate(pairs):
                fns.append((lambda l, r, i: (lambda e: e.matmul(ps, lhsT=l, rhs=r, start=(i == 0), stop=(i == n - 1))))(l, r, i))
            return P.emit("pe", fns, reads=reads, writes=q_writes)

        def mm_one(ps, l, r, start, stop, reads, writes):
            return P.emit("pe", lambda e: e.matmul(ps, lhsT=l, rhs=r, start=start, stop=stop), reads=reads, writes=writes)

        def transpose4(ps, src):
            def mk(tb):
                return lambda e: e.transpose(ps.ap[:, tb * 128:(tb + 1) * 128], src.ap[:, tb * 128:(tb + 1) * 128], ident[:])
            P.emit("pe", [mk(tb) for tb in range(4)], reads=[src.res, R_ident], writes=[ps.res])

        dma("sp", ident[:], ident_in[:, :], [], [R_ident], S_ident)
        P.emit("dve", lambda e: e.memset(onesM[:], 1.0 / 128.0), writes=[R_ones])
        P.emit("dve", lambda e: e.memset(onesD[:], 1.0 / D), writes=[R_onesD])

        def load_w(parts):
            s = wring.next()
            for (dstf, src) in parts:
                dma("pool", dstf(s.ap), src, [], [s.res], s.sem)
            return s

        def wsrc(wmat, k0, kc, c0, ncol):
            return wmat[k0 * 128:(k0 + kc) * 128, c0:c0 + ncol].rearrange("(k p) n -> p k n", p=128)

        def w_parts_k(wmat, k0, KC, c0, ncol, split=2):
            parts = []
            step = max(1, KC // split)
            for a in range(0, KC, step):
                b = min(KC, a + step)
                parts.append(((lambda a, b: (lambda sap: sap[:, a * ncol:b * ncol].rearrange("p (k n) -> p k n", n=ncol)))(a, b),
                              wsrc(wmat, k0 + a, b - a, c0, ncol)))
            return parts

        def gemm_fm(wmat, k0, KC, ncols_total, col0, a_pages, a_res, evac, prep=None, NW=256):
            nsl = ncols_total // NW
            mper = NW // 128
            jobs = [(si, m, tt) for si in range(nsl) for m in range(mper) for tt in range(TT)]
            slot = load_w(w_parts_k(wmat, k0, KC, col0, NW))
            pre_next = prep(0, 0) if prep else None
            for ji, (si, m, tt) in enumerate(jobs):
                mi = si * mper + m
                if m == 0 and tt == 0:
                    cur = slot
                    if si + 1 < nsl:
                        slot = load_w(w_parts_k(wmat, k0, KC, col0 + (si + 1) * NW, NW))
                pre = pre_next
                if prep and ji + 1 < len(jobs):
                    nj = jobs[ji + 1]
                    pre_next = prep(nj[0] * mper + nj[1], nj[2])
                ps = psring.next()
                wv = cur.ap[:, 0:KC * NW].rearrange("p (k n) -> p k n", n=NW)
                pairs = [(wv[:, k, m * 128:(m + 1) * 128], arena[:, a_pages[k], cols(tt)]) for k in range(KC)]
                mm_group(ps.ap[:], pairs, [cur.res] + [a_res[a_pages[k]] for k in range(KC)], [ps.res])
                evac(mi, tt, ps, pre)

        def phase_input():
            def ldx(j):
                tt_, c_ = divmod(j, DC)
                s_ = ldring.next()
                src = x_in[tt_ * 512:(tt_ + 1) * 512, c_ * 128:(c_ + 1) * 128].rearrange("(tb p) f -> p tb f", p=128)
                dma("sp", s_.ap[:].rearrange("p (tb f) -> p tb f", f=128), src, [], [s_.res], s_.sem)
                return s_
            PFI = 3
            xs = {j: ldx(j) for j in range(min(PFI, TT * DC))}
            for tt in range(TT):
                for c in range(DC):
                    j = tt * DC + c
                    if j + PFI < TT * DC:
                        xs[j + PFI] = ldx(j + PFI)
                    s = xs.pop(j)
                    ps = psring.next()
                    transpose4(ps, s)
                    o = evring.next()
                    act(o.ap[:], ps.ap[:], AF.Copy, [ps.res], [o.res])
                    store_tile(o, S_xres[c, :, cols(tt)], R_xres[c][tt])
                    P.emit("dve", lambda e, c=c, tt=tt, ps=ps: e.tensor_copy(out=arena[:, c, cols(tt)], in_=ps.ap[:]), reads=[ps.res], writes=[R_page[c]])

        def layer(l):
            last = (l == DEPTH - 1)
            dma("sp", vec_sb[:], vecs[l], [], [R_vec], S_vec)
            dma("pool", gw_sb[:, 0:H, :], rg_wa[l].rearrange("h i j -> i h j"), [], [R_gw], S_gw)
            dma("pool", gw_sb[:, H:2 * H, :], rg_wx[l].rearrange("h i j -> i h j"), [], [R_gw], S_gw)
            act(sct_sb[:], vec_sb[:, V_LAM:V_LAM + H], AF.Exp, [R_vec], [R_sct], scale=-1.0)
            act(sct_sb[:], sct_sb[:], AF.Ln, [R_sct], [R_sct], bias=1.0)
            ts_op(sc_sb[:], sct_sb[:], -C_RG, None, ALU.mult, None, [R_sct], [R_sc])

            allp = list(range(DC))
            HC = DC // 2
            PB = max(HC, 2 * H)
            HA_pages = list(range(0, HC))
            HB_pages = list(range(PB, PB + HC))
            assert PB + HC <= NP

            def load_half(src, rgrid, c0, pages, semi):
                ng = 2
                per = HC // ng
                for g in range(ng):
                    rd = [rgrid[c0 + g * per + i][tt] for i in range(per) for tt in range(TT)]
                    p0 = pages[0] + g * per
                    dma("sp", arena[:, p0:p0 + per, :], src[c0 + g * per:c0 + (g + 1) * per].rearrange("c p t -> p c t"),
                        rd, R_page[p0:p0 + per], asem[semi * 2 + g])

            W = w_in[l]
            def evac_qk(mi, tt, ps, pre):
                cs, sn = pre
                dst, rs = (S_q, R_q) if mi < H else (S_k, R_k)
                h = mi % H
                sw = evring.next()
                act(sw.ap[0:64, :], ps.ap[64:128, :], AF.Copy, [ps.res], [sw.res])
                act(sw.ap[64:128, :], ps.ap[0:64, :], AF.Copy, [ps.res], [sw.res])
                t1 = evring.next()
                tt_op(t1.ap[:], ps.ap[:], cs.ap[:], ALU.mult, [ps.res, cs.res], [t1.res])
                tt_op(sw.ap[:], sw.ap[:], sn.ap[:], ALU.mult, [sw.res, sn.res], [sw.res])
                ob = bfring.next()
                tt_op(ob.ap[:], t1.ap[:], sw.ap[:], ALU.add, [t1.res, sw.res], [ob.res])
                store_tile(ob, dst[h, :, cols(tt)], rs[h][tt])

            def prep_qk(mi, tt):
                return (load_tile(ldring, cos_t[:, cols(tt)], None), load_tile(ldring, sin_t[:, cols(tt)], None))

            gemm_fm(W, 0, DC, 2 * WA, 0, allp, R_page, evac_qk, prep_qk)

            if STOP == 1:
                return
            for si in range(WA // 256):
                slot = load_w(w_parts_k(W, 0, DC, 2 * WA + si * 256, 256))
                wv = slot.ap[:, 0:DC * 256].rearrange("p (k n) -> p k n", n=256)
                for tb in range(TB):
                    ps = psring.next()
                    pairs = [(arena[:, k, tb * 128:(tb + 1) * 128], wv[:, k, :]) for k in range(DC)]
                    mm_group(ps.ap[:, 0:256], pairs, [slot.res] + R_page[0:DC], [ps.res])
                    ob = bfring.next()
                    act(ob.ap[:, 0:256], ps.ap[:, 0:256], AF.Copy, [ps.res], [ob.res])
                    store_tile(ob, S_v[tb * 128:(tb + 1) * 128, si * 256:(si + 1) * 256], R_v[tb][si], ap=ob.ap[:, 0:256])

            if STOP == 2:
                return
            def evac_simple(func, dst, rs, bias_off=None):
                def ev(mi, tt, ps, pre):
                    o = evring.next()
                    act(o.ap[:], ps.ap[:], func, [ps.res] + ([R_vec] if bias_off is not None else []), [o.res],
                        bias=(vcol(bias_off, mi) if bias_off is not None else None))
                    store_tile(o, dst[mi, :, cols(tt)], rs[mi][tt])
                return ev

            gemm_fm(W, 0, DC, WA, 3 * WA, allp, R_page, evac_simple(AF.Silu, S_g, R_g))
            gemm_fm(W, 0, DC, WA, 4 * WA, allp, R_page, evac_simple(AF.Copy, S_xr, R_xr))
            gemm_fm(W, 0, DC, WA, 5 * WA, allp, R_page, evac_simple(AF.Gelu_apprx_tanh, S_gr, R_gr))
            gemm_fm(W, 0, DC, D, 6 * WA, allp, R_page, evac_simple(AF.Sigmoid, S_sga, R_sga, V_BGA))
            gemm_fm(W, 0, DC, D, 6 * WA + D, allp, R_page, evac_simple(AF.Sigmoid, S_sgb, R_sgb, V_BGB))

            if STOP == 3:
                return
            WK = 2 * H
            pg_bytes = T * 2
            work = arena[:, WK:NP, :].rearrange("p a t -> p (a t)")
            work_res = R_page[WK:NP]
            nwork = (NP - WK) * T

            def work_view(off_b, nbytes, dt, shape=None):
                e0 = off_b // 2
                ap = work[:, e0:e0 + nbytes // 2]
                if dt == F32:
                    ap = ap.bitcast(F32)
                p0 = off_b // pg_bytes
                p1 = (off_b + nbytes - 1) // pg_bytes
                return ap, work_res[p0:p1 + 1]

            def group_norm(h, it, ops, sg):
                if True:
                    o_sb = evring.next()
                    act(o_sb.ap[:], ops.ap[:], AF.Copy, [ops.res], [o_sb.res])
                    o2 = evring.next()
                    act(o2.ap[:], ops.ap[:], AF.Square, [ops.res], [o2.res])
                    mps = psb[5]
                    mm_one(mps.ap[:], onesM[:], o_sb.ap[:], True, True, [R_ones, o_sb.res], [mps.res])
                    vps = psb[6]
                    mm_one(vps.ap[:], onesM[:], o2.ap[:], True, True, [R_ones, o2.res], [vps.res])
                    mu = evring.next()
                    act(mu.ap[:], mps.ap[:], AF.Copy, [mps.res], [mu.res])
                    tt_op(o2.ap[:], mu.ap[:], mu.ap[:], ALU.mult, [mu.res], [o2.res])
                    tt_op(o2.ap[:], vps.ap[:], o2.ap[:], ALU.subtract, [vps.res, o2.res], [o2.res])
                    act(o2.ap[:], o2.ap[:], AF.Ln, [o2.res], [o2.res], bias=EPS)
                    act(o2.ap[:], o2.ap[:], AF.Exp, [o2.res], [o2.res], scale=-0.5)
                    tt_op(o_sb.ap[:], o_sb.ap[:], mu.ap[:], ALU.subtract, [o_sb.res, mu.res], [o_sb.res])
                    tt_op(o_sb.ap[:], o_sb.ap[:], o2.ap[:], ALU.mult, [o_sb.res, o2.res], [o_sb.res])
                    stt_op(arena[:, h, cols(it)], o_sb.ap[:], vcol(V_GN, h), sg.ap[:], ALU.mult, ALU.mult,
                           [o_sb.res, sg.res, R_vec], [R_page[h]])

            pending = None
            opc = 0
            spc = 0
            for h in range(H):
                par = h % 2
                base = par * (3 * T * 2 + MW * 4 + 128)
                base = (base + 63) // 64 * 64
                qT, rq = work_view(base, T * 2, BF16)
                kT, rk = work_view(base + T * 2, T * 2, BF16)
                vv, rv = work_view(base + 2 * T * 2, T * 2, BF16)
                mk, rm = work_view(base + 3 * T * 2, MW * 4, F32)
                dma("sp", qT, S_q[h], R_q[h], rq, gring.next())
                dma("sp", kT, S_k[h], R_k[h], rk, gring.next())
                vsrc = S_v[:, h * 128:(h + 1) * 128].rearrange("(tb p) e -> p tb e", p=128)
                dma("sp", vv.rearrange("p (tb e) -> p tb e", e=128), vsrc, [R_v[tb][h // 2] for tb in range(TB)], rv, gring.next())
                dma("sp", mk, mask_t[h], [], rm, gring.next())
                for it in range(TT):
                    ops = psb[opc % 2]
                    opc += 1
                    jmax = min(4 * it + 3, TB - 1)
                    sg = load_tile(ldring, S_g[h, :, cols(it)], R_g[h][it])

                    def s_mm(jb, it=it, kT=kT, qT=qT, rq=rq, rk=rk):
                        nonlocal spc
                        sps = psb[2 + (spc % 3)]
                        spc += 1
                        mm_one(sps.ap[:], kT[:, jb * 128:(jb + 1) * 128], qT[:, cols(it)], True, True, rq + rk, [sps.res])
                        return sps

                    nxt = s_mm(0)
                    for jb in range(jmax + 1):
                        sps = nxt
                        if jb + 1 <= jmax:
                            nxt = s_mm(jb + 1)
                        off = it * 512 - jb * 128 + 384
                        pt = bfring.next()
                        tt_op(pt.ap[:], sps.ap[:], mk[:, off:off + 512], ALU.mult, [sps.res] + rm, [pt.res])
                        mm_one(ops.ap[:], vv[:, jb * 128:(jb + 1) * 128], pt.ap[:], jb == 0, jb == jmax, rv + [pt.res], [ops.res])
                    if pending is not None:
                        pending()
                    pending = (lambda h=h, it=it, ops=ops, sg=sg: group_norm(h, it, ops, sg))

            if pending is not None:
                pending()

            if STOP == 4:
                return
            for c in range(H):
                TP = T + 4
                nb = 0
                def wv_(nbytes, dt):
                    nonlocal nb
                    ap, rs = work_view(nb, nbytes, dt)
                    nb += (nbytes + 63) // 64 * 64
                    return ap, rs
                xrp, r_xrp = wv_(TP * 4, F32)
                xc, r_xc = wv_(T * 4, F32)
                av, r_a = wv_(T * 4, F32)
                ig, r_ig = wv_(T * 4, F32)
                tmp, r_tmp = wv_(T * 4, F32)
                gg, r_gg = wv_(T * 4, F32)
                hh, r_hh = wv_(T * 4, F32)
                xcb, r_xcb = wv_(T * 2, BF16)
                assert nb <= nwork * 2, (nb, nwork * 2)
                P.emit("dve", lambda e, xrp=xrp: e.memset(xrp[:, 0:4], 0.0), writes=r_xrp)
                dma("sp", xrp[:, 4:4 + T], S_xr[c], R_xr[c], r_xrp, gring.next())
                dma("sp", gg, S_gr[c], R_gr[c], r_gg, gring.next())
                ts_op(xc, xrp[:, 1:1 + T], vcol(V_CW, 0 * H + c), vcol(V_CB, c), ALU.mult, ALU.add, r_xrp + [R_vec], r_xc)
                for tap in range(1, 4):
                    stt_op(xc, xrp[:, 1 + tap:1 + tap + T], vcol(V_CW, tap * H + c), xc, ALU.mult, ALU.add, r_xrp + r_xc + [R_vec], r_xc)
                act(xcb, xc, AF.Copy, r_xc, r_xcb)
                for tt in range(TT):
                    rps = psring.next()
                    mm_one(rps.ap[:], gw_sb[:, c, :], xcb[:, cols(tt)], True, True, [R_gw] + r_xcb, [rps.res])
                    ips = psring.next()
                    mm_one(ips.ap[:], gw_sb[:, H + c, :], xcb[:, cols(tt)], True, True, [R_gw] + r_xcb, [ips.res])
                    act(av[:, cols(tt)], rps.ap[:], AF.Sigmoid, [rps.res, R_vec], r_a, bias=vcol(V_BA, c))
                    act(ig[:, cols(tt)], ips.ap[:], AF.Sigmoid, [ips.res, R_vec], r_ig, bias=vcol(V_BX, c))
                act(av, av, AF.Exp, r_a + [R_sc], r_a, scale=sc_sb[:, c:c + 1])
                tt_op(tmp, av, av, ALU.mult, r_a, r_tmp)
                ts_op(tmp, tmp, -1.0, 1.0, ALU.mult, ALU.add, r_tmp, r_tmp)
                ts_op(tmp, tmp, 1e-30, None, ALU.max, None, r_tmp, r_tmp)
                act(tmp, tmp, AF.Sqrt, r_tmp, r_tmp)
                tt_op(ig, ig, xc, ALU.mult, r_ig + r_xc, r_ig)
                tt_op(ig, ig, tmp, ALU.mult, r_ig + r_tmp, r_ig)
                P.emit("dve", lambda e, hh=hh, av=av, ig=ig: e.tensor_tensor_scan(out=hh, data0=av, data1=ig, initial=0.0, op0=ALU.mult, op1=ALU.add),
                       reads=r_a + r_ig, writes=r_hh)
                tt_op(arena[:, H + c, :], hh, gg, ALU.mult, r_hh + r_gg, [R_page[H + c]])

            if STOP == 5:
                return
            KC4 = 2 * H
            NW4 = 8192 // KC4 if 8192 // KC4 <= 512 else 512
            NW4 = min(NW4, D)
            nsl = D // NW4
            mper = NW4 // 128

            def load_w4(si):
                s = wring.next()
                for half, wm in ((0, w_bra[l]), (1, w_brb[l])):
                    dst = s.ap[:, half * H * NW4:(half + 1) * H * NW4].rearrange("p (k n) -> p k n", n=NW4)
                    dma("pool", dst, wsrc(wm, 0, H, si * NW4, NW4), [], [s.res], s.sem)
                return s

            slot = load_w4(0)
            jobs4 = [(si, m, tt) for si in range(nsl) for m in range(mper) for tt in range(TT)]

            def prep4(j):
                si_, m_, tt_ = jobs4[j]
                mi_ = si_ * mper + m_
                return (load_tile(ldring, S_sga[mi_, :, cols(tt_)], R_sga[mi_][tt_]),
                        load_tile(ldring, S_sgb[mi_, :, cols(tt_)], R_sgb[mi_][tt_]))

            pre4 = prep4(0)
            for si in range(nsl):
                cur = slot
                if si + 1 < nsl:
                    slot = load_w4(si + 1)
                wv = cur.ap[:, 0:KC4 * NW4].rearrange("p (k n) -> p k n", n=NW4)
                for m in range(mper):
                    mi = si * mper + m
                    for tt in range(TT):
                        ji = (si * mper + m) * TT + tt
                        sa, sbt = pre4
                        if ji + 1 < len(jobs4):
                            pre4 = prep4(ji + 1)
                        pa = psring.next()
                        mm_group(pa.ap[:], [(wv[:, k, m * 128:(m + 1) * 128], arena[:, k, cols(tt)]) for k in range(H)],
                                 [cur.res] + R_page[0:H], [pa.res])
                        pb = psring.next()
                        mm_group(pb.ap[:], [(wv[:, H + k, m * 128:(m + 1) * 128], arena[:, H + k, cols(tt)]) for k in range(H)],
                                 [cur.res] + R_page[H:2 * H], [pb.res])
                        tt_op(sa.ap[:], sa.ap[:], pa.ap[:], ALU.mult, [sa.res, pa.res], [sa.res])
                        tt_op(sbt.ap[:], sbt.ap[:], pb.ap[:], ALU.mult, [sbt.res, pb.res], [sbt.res])
                        if mi < HC:
                            tt_op(arena[:, PB + mi, cols(tt)], sa.ap[:], sbt.ap[:], ALU.add, [sa.res, sbt.res], [R_page[PB + mi]])
                        else:
                            ob = bfring.next()
                            tt_op(ob.ap[:], sa.ap[:], sbt.ap[:], ALU.add, [sa.res, sbt.res], [ob.res])
                            store_tile(ob, S_mg[mi, :, cols(tt)], R_mg[mi][tt])

            if STOP == 6:
                return
            def prep_res(mi, tt):
                return load_tile(ldring, S_xres[mi, :, cols(tt)], R_xres[mi][tt])

            def acc_stats(mi, tt, o):
                if mi == 0:
                    P.emit("dve", lambda e: e.tensor_copy(out=accS[tt].ap[:], in_=o.ap[:]), reads=[o.res], writes=[accS[tt].res])
                    act(accQ[tt].ap[:], o.ap[:], AF.Square, [o.res], [accQ[tt].res])
                else:
                    tt_op(accS[tt].ap[:], accS[tt].ap[:], o.ap[:], ALU.add, [accS[tt].res, o.res], [accS[tt].res])
                    sq = evring.next()
                    act(sq.ap[:], o.ap[:], AF.Square, [o.res], [sq.res])
                    tt_op(accQ[tt].ap[:], accQ[tt].ap[:], sq.ap[:], ALU.add, [accQ[tt].res, sq.res], [accQ[tt].res])

            def evac_res_f(with_stats):
                def ev(mi, tt, ps, pre):
                    o = evring.next()
                    stt_op(o.ap[:], pre.ap[:], ALPHA, ps.ap[:], ALU.mult, ALU.add, [pre.res, ps.res], [o.res])
                    if with_stats:
                        acc_stats(mi, tt, o)
                    store_tile(o, S_z[mi, :, cols(tt)], R_z[mi][tt])
                return ev

            def prep_z(mi, tt):
                return load_tile(ldring, S_z[mi, :, cols(tt)], R_z[mi][tt])

            def evac_z_f(with_stats):
                def ev(mi, tt, ps, pre):
                    o = evring.next()
                    tt_op(o.ap[:], pre.ap[:], ps.ap[:], ALU.add, [pre.res, ps.res], [o.res])
                    if with_stats:
                        acc_stats(mi, tt, o)
                    store_tile(o, S_z[mi, :, cols(tt)], R_z[mi][tt])
                return ev

            NWH = 512 if (HC * 512 <= 8192 and D % 512 == 0) else 256
            load_half(S_mg, R_mg, HC, HA_pages, 0)
            gemm_fm(w_o[l], 0, HC, D, 0, HB_pages, R_page, evac_res_f(False), prep_res, NW=NWH)
            gemm_fm(w_o[l], HC, HC, D, 0, HA_pages, R_page, evac_z_f(True), prep_z, NW=NWH)

            if STOP == 7:
                return
            def layer_norm(goff, boff, final):
                for tt in range(TT):
                    mps = psring.next()
                    vps = psring.next()
                    mm_one(mps.ap[:], onesD[:], accS[tt].ap[:], True, True, [R_onesD, accS[tt].res], [mps.res])
                    mm_one(vps.ap[:], onesD[:], accQ[tt].ap[:], True, True, [R_onesD, accQ[tt].res], [vps.res])
                    mu = accS[tt]
                    rstd = accQ[tt]
                    act(mu.ap[:], mps.ap[:], AF.Copy, [mps.res], [mu.res])
                    tt_op(rstd.ap[:], mu.ap[:], mu.ap[:], ALU.mult, [mu.res], [rstd.res])
                    tt_op(rstd.ap[:], vps.ap[:], rstd.ap[:], ALU.subtract, [vps.res, rstd.res], [rstd.res])
                    act(rstd.ap[:], rstd.ap[:], AF.Sqrt, [rstd.res], [rstd.res], bias=EPS)
                    P.emit("dve", lambda e, r=rstd: e.reciprocal(out=r.ap[:], in_=r.ap[:]), reads=[rstd.res], writes=[rstd.res])
                for tt in range(TT):
                    mu = accS[tt]
                    rstd = accQ[tt]
                    PF = 3
                    zs = {}
                    for c in range(min(PF, DC)):
                        zs[c] = load_tile(ldring, S_z[c, :, cols(tt)], R_z[c][tt])
                    for c in range(DC):
                        if c + PF < DC:
                            zs[c + PF] = load_tile(ldring, S_z[c + PF, :, cols(tt)], R_z[c + PF][tt])
                        z = zs.pop(c)
                        tt_op(z.ap[:], z.ap[:], mu.ap[:], ALU.subtract, [z.res, mu.res], [z.res])
                        tt_op(z.ap[:], z.ap[:], rstd.ap[:], ALU.mult, [z.res, rstd.res], [z.res])
                        o = evring.next()
                        act(o.ap[:], z.ap[:], AF.Identity, [z.res, R_vec], [o.res], bias=vcol(boff, c), scale=vcol(goff, c))
                        if not final:
                            store_tile(o, S_xres[c, :, cols(tt)], R_xres[c][tt])
                            act(arena[:, c, cols(tt)], o.ap[:], AF.Copy, [o.res], [R_page[c]])
                        else:
                            ps = psring.next()
                            transpose4(ps, o)
                            o2 = evring.next()
                            act(o2.ap[:], ps.ap[:], AF.Copy, [ps.res], [o2.res])
                            dst = y_out[tt * 512:(tt + 1) * 512, c * 128:(c + 1) * 128].rearrange("(tb p) f -> p tb f", p=128)
                            P.emit("sp", lambda e, dst=dst, o2=o2: e.dma_start(out=dst, in_=o2.ap[:].rearrange("p (tb f) -> p tb f", f=128)),
                                   reads=[o2.res], writes=[], dsem=o2.sem)

            layer_norm(V_LN1G, V_LN1B, False)

            if STOP == 8:
                return
            def evac_up(mi, tt, ps, pre):
                o = evring.next()
                act(o.ap[:], ps.ap[:], AF.Relu, [ps.res], [o.res])
                ob = bfring.next()
                tt_op(ob.ap[:], o.ap[:], o.ap[:], ALU.mult, [o.res], [ob.res])
                store_tile(ob, S_h[mi, :, cols(tt)], R_h[mi][tt])

            gemm_fm(w_up[l], 0, DC, cfg.DFF, 0, allp, R_page, evac_up)

            if STOP == 9:
                return
            KB2 = FC // HC
            halves = [HA_pages, HB_pages]
            load_half(S_h, R_h, 0, halves[0], 0)
            for kb2 in range(KB2):
                if kb2 + 1 < KB2:
                    load_half(S_h, R_h, (kb2 + 1) * HC, halves[(kb2 + 1) % 2], (kb2 + 1) % 2)
                lastk = (kb2 == KB2 - 1)
                if kb2 == 0:
                    gemm_fm(w_dn[l], 0, HC, D, 0, halves[0], R_page, evac_res_f(lastk), prep_res, NW=NWH)
                else:
                    gemm_fm(w_dn[l], kb2 * HC, HC, D, 0, halves[kb2 % 2], R_page, evac_z_f(lastk), prep_z, NW=NWH)

            layer_norm(V_LN2G, V_LN2B, last)

        phase_input()
        for l in range(DEPTH):
            if STOP == 0:
                break
            layer(l)
        P.final_wait("sp", list(P.dma_cnt.keys()))

        esems = {q: es.enter_context(nc.semaphore(f"sem_{q}")) for q in Planner.QS}
        dsems = {k: es.enter_context(nc.semaphore(f"dsem_{k}")) for k in P.dma_cnt}
        block = es.enter_context(nc.Block())
        P.replay(nc, block, esems, dsems)
    return nc


def host_consts(cfg):
    T, H, MW = cfg.T, cfg.H, cfg.MW
    f32 = np.float32
    pos = np.arange(T, dtype=f32)
    inv_freq = (f32(ROPE_BASE) ** (-np.arange(0, 128, 2, dtype=f32) / f32(128))).astype(f32)
    ang = (pos[:, None] * inv_freq[None, :]).astype(f32)
    cos = np.cos(ang).astype(f32).T
    sin = np.sin(ang).astype(f32).T
    cos_t = np.concatenate([cos, cos], 0)
    sin_t = np.concatenate([-sin, sin], 0)
    lg = np.log1p(-np.exp2(-5.0 - np.arange(H, dtype=f32))).astype(f32)
    jl = np.arange(128)[:, None]
    r = np.arange(MW)[None, :] - 384
    ci = np.floor_divide(r, CHUNK)
    cj = jl // CHUNK
    valid = ci >= cj
    dist = np.abs(r - jl).astype(f32)
    mask = np.zeros((H, 128, MW), f32)
    for h in range(H):
        mask[h] = np.where(valid, np.exp(lg[h] * dist) * f32(128 ** -0.5), 0.0)
    return (np.ascontiguousarray(cos_t), np.ascontiguousarray(sin_t), mask, np.eye(128, dtype=f32))


def host_vecs(cfg, ins):
    DC, H, DEPTH = cfg.DC, cfg.H, cfg.DEPTH
    NV = 4 * DC + 2 * DC + 10 * H
    out = np.zeros((DEPTH, 128, NV), np.float32)

    def pc(v, n):
        return np.asarray(v, np.float32).reshape(n, 128).T

    for l in range(DEPTH):
        o = 0
        for name in ("ln1_g", "ln1_b", "ln2_g", "ln2_b"):
            out[l, :, o:o + DC] = pc(ins[name][l], DC); o += DC
        bg = np.asarray(ins["b_gate"][l], np.float32)
        out[l, :, o:o + DC] = pc(bg[:cfg.D], DC); o += DC
        out[l, :, o:o + DC] = pc(bg[cfg.D:], DC); o += DC
        out[l, :, o:o + H] = pc(ins["ret_gn_w"][l], H); o += H
        for tap in range(4):
            out[l, :, o:o + H] = pc(ins["conv_w"][l][tap], H); o += H
        for name in ("conv_b", "rg_ba", "rg_bx", "rg_lambda"):
            out[l, :, o:o + H] = pc(ins[name][l], H); o += H
    return out


_NC_CACHE = {}


def run_cfg(cfg, ins, n_cores, trace=False):
    key = (cfg.D, cfg.H, cfg.DFF, cfg.T, cfg.DEPTH)
    if key not in _NC_CACHE:
        _NC_CACHE[key] = build_program(cfg)
    nc = _NC_CACHE[key]
    cos_t, sin_t, mask, ident = host_consts(cfg)
    vecs = host_vecs(cfg, ins)
    shared = {
        "w_in": np.ascontiguousarray(ins["w_in"], dtype=np.float32),
        "w_br_a": np.ascontiguousarray(ins["w_br_a"], dtype=np.float32),
        "w_br_b": np.ascontiguousarray(ins["w_br_b"], dtype=np.float32),
        "w_o": np.ascontiguousarray(ins["w_o"], dtype=np.float32),
        "w_up": np.ascontiguousarray(ins["w_up"], dtype=np.float32),
        "w_down": np.ascontiguousarray(ins["w_down"], dtype=np.float32),
        "rg_wa": np.ascontiguousarray(ins["rg_wa"], dtype=np.float32),
        "rg_wx": np.ascontiguousarray(ins["rg_wx"], dtype=np.float32),
        "vecs": vecs, "cos_t": cos_t, "sin_t": sin_t, "mask_t": mask, "ident": ident,
    }
    x = np.asarray(ins["x"], dtype=np.float32)
    in_maps = []
    for c in range(n_cores):
        m = dict(shared)
        m["x"] = np.ascontiguousarray(x[c])
        in_maps.append(m)
    res = run_bass_kernel_spmd(nc, in_maps, core_ids=list(range(n_cores)), **({"trace": True} if trace else {}))
    out = np.stack([np.asarray(r["y"], dtype=np.float32) for r in res.results], 0)
    return out, res


def kernel(x, w_in, ret_gn_w, conv_w, conv_b, rg_wa, rg_ba, rg_wx, rg_bx, rg_lambda, w_br_a, w_br_b,
           b_gate, w_o, ln1_g, ln1_b, w_up, w_down, ln2_g, ln2_b):
    cfg = Cfg()
    ins = dict(x=x, w_in=w_in, ret_gn_w=ret_gn_w, conv_w=conv_w, conv_b=conv_b, rg_wa=rg_wa, rg_ba=rg_ba,
               rg_wx=rg_wx, rg_bx=rg_bx, rg_lambda=rg_lambda, w_br_a=w_br_a, w_br_b=w_br_b, b_gate=b_gate,
               w_o=w_o, ln1_g=ln1_g, ln1_b=ln1_b, w_up=w_up, w_down=w_down, ln2_g=ln2_g, ln2_b=ln2_b)
    ins = {k: np.asarray(v) for k, v in ins.items()}
    out, _ = run_cfg(cfg, ins, 8)
    return out.astype(np.float32)
```

```python
import numpy as np
import ml_dtypes
import concourse.bass as bass
import concourse.mybir as mybir
from concourse.bass_utils import run_bass_kernel_spmd

F32 = mybir.dt.float32
BF16 = mybir.dt.bfloat16
AF = mybir.ActivationFunctionType
ALU = mybir.AluOpType

CHUNK = 64
EPS = 1e-5
C_RG = 8.0
ROPE_BASE = 10000.0


class Cfg:
    def __init__(self, D=4096, H=8, DFF=16384, T=2048, DEPTH=4):
        self.D, self.H, self.DFF, self.T, self.DEPTH = D, H, DFF, T, DEPTH
        self.DC = D // 128
        self.FC = DFF // 128
        self.TT = T // 512
        self.TB = T // 128
        self.WA = 128 * H
        self.INC = 4 * self.WA + 2 * self.WA + 2 * D
        self.MW = T + 384
        self.ALPHA = (2.0 * DEPTH) ** 0.25
        self.KB = self.FC // self.DC
        need = max(2 * (3 * T * 2 + self.MW * 4 + 128), 7 * (T * 4 + 64) + T * 2 + 64)
        self.NP = max(self.DC, 2 * H + -(-need // (T * 2)))
        assert self.FC % self.DC == 0 and T % 512 == 0 and self.DC % 4 == 0


class Res:
    __slots__ = ("w", "r", "name", "excl")

    def __init__(self, name="", excl=False):
        self.w = {}
        self.r = {}
        self.name = name
        self.excl = excl


class Item:
    __slots__ = ("waits", "fns", "ev", "dma")

    def __init__(self, waits, fns, ev, dma):
        self.waits, self.fns, self.ev, self.dma = waits, fns, ev, dma


class Planner:
    QS = ("pe", "act", "dve", "pool", "sp")

    def __init__(self):
        self.items = {q: [] for q in self.QS}
        self.known = {q: {} for q in self.QS}
        self.dma_cnt = {}
        self.needed = {q: set() for q in self.QS}

    def new_dsem(self, key):
        self.dma_cnt[key] = 0
        return key

    def emit(self, q, fns, reads=(), writes=(), dsem=None):
        if not isinstance(fns, (list, tuple)):
            fns = [fns]
        ex = [r for r in reads if r.excl]
        if ex:
            reads = [r for r in reads if not r.excl]
            writes = list(writes) + ex
        deps = {}
        for r in reads:
            for k, v in r.w.items():
                if deps.get(k, -1) < v:
                    deps[k] = v
        for w in writes:
            for k, v in w.w.items():
                if deps.get(k, -1) < v:
                    deps[k] = v
            for k, v in w.r.items():
                if deps.get(k, -1) < v:
                    deps[k] = v
        kn = self.known[q]
        waits = []
        idx = len(self.items[q])
        for k, v in deps.items():
            if k == q and q == "pe" and dsem is None:
                continue
            if kn.get(k, -1) >= v:
                continue
            kn[k] = v
            waits.append((k, v))
            if k in self.needed:
                self.needed[k].add(v)
        if dsem is not None:
            self.dma_cnt[dsem] += 1
            ev = (dsem, self.dma_cnt[dsem])
        else:
            ev = (q, idx)
        self.items[q].append(Item(waits, fns, ev, dsem is not None))
        for r in reads:
            if r.r.get(ev[0], -1) < ev[1]:
                r.r[ev[0]] = ev[1]
        for w in writes:
            w.w[ev[0]] = ev[1]
        return ev

    def final_wait(self, q, keys):
        waits = []
        for k in keys:
            v = self.dma_cnt[k]
            if v > 0 and self.known[q].get(k, -1) < v:
                waits.append((k, v))
        self.items[q].append(Item(waits, [], None, False))

    def replay(self, nc, block, esems, dsems):
        val = {}
        for q in self.QS:
            nd = sorted(self.needed[q])
            val[q] = {idx: i + 1 for i, idx in enumerate(nd)}

        def run(q, eng):
            vq = val[q]
            for idx, it in enumerate(self.items[q]):
                for (k, v) in it.waits:
                    if k in esems:
                        eng.wait_ge(esems[k], val[k][v])
                    else:
                        eng.wait_ge(dsems[k], 16 * v)
                ins = None
                for f in it.fns:
                    ins = f(eng)
                if ins is None:
                    continue
                if it.dma:
                    ins.then_inc(dsems[it.ev[0]], 16)
                elif idx in vq:
                    ins.then_inc(esems[q], 1)

        @block.tensor
        def _(e):
            run("pe", e)

        @block.scalar
        def _(e):
            run("act", e)

        @block.vector
        def _(e):
            run("dve", e)

        @block.gpsimd
        def _(e):
            run("pool", e)

        @block.sync
        def _(e):
            run("sp", e)


class Slot:
    __slots__ = ("ap", "res", "sem")

    def __init__(self, ap, res, sem):
        self.ap, self.res, self.sem = ap, res, sem


class Ring:
    def __init__(self, slots):
        self.slots = slots
        self.i = 0

    def next(self):
        s = self.slots[self.i % len(self.slots)]
        self.i += 1
        return s


def build_program(cfg):
    D, H, T, DC, FC, TT, TB, WA = cfg.D, cfg.H, cfg.T, cfg.DC, cfg.FC, cfg.TT, cfg.TB, cfg.WA
    DEPTH, INC, MW, ALPHA, KB = cfg.DEPTH, cfg.INC, cfg.MW, cfg.ALPHA, cfg.KB
    import os
    STOP = int(os.environ.get("KSTOP", "-1"))
    nc = bass.Bass("TRN2", target_bir_lowering=False)
    P = Planner()

    def din(name, shape, dt=F32):
        return nc.dram_tensor(name, list(shape), dt, kind="ExternalInput").ap()

    def dscr(name, shape, dt=F32):
        return nc.dram_tensor(name, list(shape), dt, kind="Internal").ap()

    x_in = din("x", [T, D])
    w_in = din("w_in", [DEPTH, D, INC])
    w_bra = din("w_br_a", [DEPTH, WA, D])
    w_brb = din("w_br_b", [DEPTH, WA, D])
    w_o = din("w_o", [DEPTH, D, D])
    w_up = din("w_up", [DEPTH, D, cfg.DFF])
    w_dn = din("w_down", [DEPTH, cfg.DFF, D])
    rg_wa = din("rg_wa", [DEPTH, H, 128, 128])
    rg_wx = din("rg_wx", [DEPTH, H, 128, 128])
    NV = 4 * DC + 2 * DC + 10 * H
    vecs = din("vecs", [DEPTH, 128, NV])
    cos_t = din("cos_t", [128, T])
    sin_t = din("sin_t", [128, T])
    mask_t = din("mask_t", [H, 128, MW])
    ident_in = din("ident", [128, 128])
    y_out = nc.dram_tensor("y", [T, D], F32, kind="ExternalOutput").ap()

    S_xres = dscr("S_xres", [DC, 128, T])
    S_z = dscr("S_z", [DC, 128, T])
    S_q = dscr("S_q", [H, 128, T], BF16)
    S_k = dscr("S_k", [H, 128, T], BF16)
    S_v = dscr("S_v", [T, WA], BF16)
    S_g = dscr("S_g", [H, 128, T])
    S_xr = dscr("S_xr", [H, 128, T])
    S_gr = dscr("S_gr", [H, 128, T])
    S_sga = dscr("S_sga", [DC, 128, T])
    S_sgb = dscr("S_sgb", [DC, 128, T])
    S_mg = dscr("S_mg", [DC, 128, T], BF16)
    S_h = dscr("S_h", [FC, 128, T], BF16)
    S_xb = dscr("S_xb", [DC, 128, T // 2], BF16)
    S_ya = dscr("S_ya", [H, 128, T], BF16)
    S_yb = dscr("S_yb", [H, 128, T], BF16)

    def grid(name, n1, n2):
        return [[Res(f"{name}{i}_{j}") for j in range(n2)] for i in range(n1)]

    R_xres, R_z = grid("xres", DC, TT), grid("z", DC, TT)
    R_q, R_k, R_g, R_xr, R_gr = (grid(n, H, TT) for n in ("q", "k", "g", "xr", "gr"))
    R_v = grid("v", TB, WA // 256)
    R_sga, R_sgb, R_mg = grid("sga", DC, TT), grid("sgb", DC, TT), grid("mg", DC, TT)
    R_h = grid("h", FC, TT)

    import contextlib
    es = contextlib.ExitStack()
    with es:
        def sb(name, shape, dt=F32):
            return es.enter_context(nc.sbuf_tensor(name, list(shape), dt))

        HC = DC // 2
        TTH = TT // 2
        TH = T // 2
        assert TT % 2 == 0
        need_work = max(2 * (3 * T * 2 + MW * 4 + 128), 2 * ((T + 4) * 4 + 64) + 5 * T * 4 + T * 2 + 512)
        if need_work <= HC * T * 2 and 2 * H <= HC:
            WP0, NWP, YAB0, NP = 0, HC, HC, DC
        else:
            YAB0 = DC
            WP0 = DC + 2 * H
            NWP = -(-need_work // (T * 2))
            NP = WP0 + NWP
        arena = sb("arena", [128, NP, T], BF16)
        R_page = [Res(f"page{i}") for i in range(NP)]
        xh = [arena[:, sidx * HC:(sidx + 1) * HC, :].rearrange("p a t -> p (a t)").rearrange("p (c t) -> p c t", t=TH) for sidx in range(2)]

        def x_ap(k, tt):
            o = (tt % TTH) * 512
            return xh[tt // TTH][:, k, o:o + 512]

        def x_res(k, tt):
            return R_page[(tt // TTH) * HC + k // 2]

        xbsems = [P.new_dsem(f"xb{i}") for i in range(4)]
        R_xbchain = [Res(f"xbchain{i}") for i in range(4)]
        xbcnt = [0]
        R_xb = [Res(f"xb{i}") for i in range(DC)]
        R_ya = [[Res(f"ya{i}_{j}") for j in range(TT)] for i in range(H)]
        R_yb = [Res(f"yb{i}") for i in range(H)]

        def store_xb(c, tt):
            if tt < TTH:
                o = (tt % TTH) * 512
                i = xbcnt[0] % 4
                xbcnt[0] += 1
                dma("sp", S_xb[c, :, o:o + 512], x_ap(c, tt), [x_res(c, tt)], [R_xb[c], R_xbchain[i]], xbsems[i])
        WSL = 8192
        wslots = [Slot(sb(f"wslot{i}", [128, WSL], BF16), Res(f"wslot{i}"), P.new_dsem(f"w{i}")) for i in range(2)]
        wring = Ring(wslots)
        NLD, NEV, NBF = 5, 5, 4
        ldring = Ring([Slot(sb(f"ld{i}", [128, 512]), Res(f"ld{i}"), P.new_dsem(f"ld{i}")) for i in range(NLD)])
        evring = Ring([Slot(sb(f"ev{i}", [128, 512]), Res(f"ev{i}"), P.new_dsem(f"ev{i}")) for i in range(NEV)])
        bfring = Ring([Slot(sb(f"bf{i}", [128, 512], BF16), Res(f"bf{i}"), P.new_dsem(f"bf{i}")) for i in range(NBF)])
        accS = [Slot(sb(f"accS{i}", [128, 512]), Res(f"accS{i}"), None) for i in range(TT)]
        accQ = [Slot(sb(f"accQ{i}", [128, 512]), Res(f"accQ{i}"), None) for i in range(TT)]
        ident = sb("ident_sb", [128, 128]); R_ident = Res("ident"); S_ident = P.new_dsem("ident")
        onesM = sb("onesM", [128, 128]); R_ones = Res("ones")
        onesD = sb("onesD", [128, 128]); R_onesD = Res("onesD")
        onesMb = sb("onesMb", [128, 128], BF16); R_onesb = Res("onesMb")
        vec_sb = sb("vec_sb", [128, NV]); R_vec = Res("vec"); S_vec = P.new_dsem("vec")
        gw_sb = sb("gw_sb", [128, 2 * H, 128], BF16); R_gw = Res("gw"); S_gw = P.new_dsem("gw")
        sc_sb = sb("sc_sb", [128, H]); R_sc = Res("sc")
        sct_sb = sb("sct_sb", [128, H]); R_sct = Res("sct")
        NPS = 8
        psb = [Slot(es.enter_context(nc.psum_tensor(f"ps{i}", [128, 512], F32)), Res(f"ps{i}", excl=True), None) for i in range(NPS)]
        psring = Ring(psb)
        asem = [P.new_dsem(f"a{i}") for i in range(4)]
        gsem = [P.new_dsem(f"g{i}") for i in range(8)]
        gring = Ring(gsem)
        osem = P.new_dsem("out")

        V_LN1G, V_LN1B, V_LN2G, V_LN2B = 0, DC, 2 * DC, 3 * DC
        V_BGA, V_BGB = 4 * DC, 5 * DC
        V_GN = 6 * DC
        V_CW = V_GN + H
        V_CB = V_CW + 4 * H
        V_BA = V_CB + H
        V_BX = V_BA + H
        V_LAM = V_BX + H

        def vcol(off, i):
            return vec_sb[:, off + i:off + i + 1]

        def cols(tt):
            return slice(tt * 512, (tt + 1) * 512)

        def dma(q, out, in_, reads, writes, sem):
            return P.emit(q, lambda e: e.dma_start(out=out, in_=in_), reads=reads, writes=writes, dsem=sem)

        def load_tile(ring, src_ap, src_res, q="sp"):
            s = ring.next()
            dma(q, s.ap[:], src_ap, [src_res] if src_res is not None else [], [s.res], s.sem)
            return s

        def store_tile(s, dst_ap, dst_res, q="sp", ap=None):
            dma(q, dst_ap, s.ap[:] if ap is None else ap, [s.res], [dst_res], s.sem)

        def act(out, in_, func, reads, writes, bias=None, scale=None):
            kw = {}
            if bias is not None:
                kw["bias"] = bias
            if scale is not None:
                kw["scale"] = scale
            return P.emit("act", lambda e: e.activation(out=out, in_=in_, func=func, **kw), reads=reads, writes=writes)

        def tt_op(out, in0, in1, op, reads, writes, q="dve"):
            return P.emit(q, lambda e: e.tensor_tensor(out=out, in0=in0, in1=in1, op=op), reads=reads, writes=writes)

        def ts_op(out, in0, s1, s2, op0, op1, reads, writes, q="dve"):
            if op1 is None:
                return P.emit(q, lambda e: e.tensor_scalar(out=out, in0=in0, scalar1=s1, scalar2=None, op0=op0), reads=reads, writes=writes)
            return P.emit(q, lambda e: e.tensor_scalar(out=out, in0=in0, scalar1=s1, scalar2=s2, op0=op0, op1=op1), reads=reads, writes=writes)

        def stt_op(out, in0, scalar, in1, op0, op1, reads, writes):
            return P.emit("dve", lambda e: e.scalar_tensor_tensor(out=out, in0=in0, scalar=scalar, in1=in1, op0=op0, op1=op1), reads=reads, writes=writes)

        def mm_group(ps, pairs, reads, q_writes):
            n = len(pairs)
            fns = []
            for i, (l, r) in enumerate(pairs):
                fns.append((lambda l, r, i: (lambda e: e.matmul(ps, lhsT=l, rhs=r, start=(i == 0), stop=(i == n - 1))))(l, r, i))
            return P.emit("pe", fns, reads=reads, writes=q_writes)

        def mm_one(ps, l, r, start, stop, reads, writes):
            return P.emit("pe", lambda e: e.matmul(ps, lhsT=l, rhs=r, start=start, stop=stop), reads=reads, writes=writes)

        def transpose4(ps, src):
            def mk(tb):
                return lambda e: e.transpose(ps.ap[:, tb * 128:(tb + 1) * 128], src.ap[:, tb * 128:(tb + 1) * 128], ident[:])
            P.emit("pe", [mk(tb) for tb in range(4)], reads=[src.res, R_ident], writes=[ps.res])

        dma("sp", ident[:], ident_in[:, :], [], [R_ident], S_ident)
        P.emit("dve", lambda e: e.memset(onesM[:], 1.0 / 128.0), writes=[R_ones])
        P.emit("dve", lambda e: e.memset(onesD[:], 1.0 / D), writes=[R_onesD])
        P.emit("dve", lambda e: e.memset(onesMb[:], 1.0 / 128.0), writes=[R_onesb])

        def load_w(parts):
            s = wring.next()
            for (dstf, src) in parts:
                dma("pool", dstf(s.ap), src, [], [s.res], s.sem)
            return s

        def wsrc(wmat, k0, kc, c0, ncol):
            return wmat[k0 * 128:(k0 + kc) * 128, c0:c0 + ncol].rearrange("(k p) n -> p k n", p=128)

        def w_parts_k(wmat, k0, KC, c0, ncol, split=2):
            parts = []
            step = max(1, KC // split)
            for a in range(0, KC, step):
                b = min(KC, a + step)
                parts.append(((lambda a, b: (lambda sap: sap[:, a * ncol:b * ncol].rearrange("p (k n) -> p k n", n=ncol)))(a, b),
                              wsrc(wmat, k0 + a, b - a, c0, ncol)))
            return parts

        def gemm_fm(wmat, k0, KC, ncols_total, col0, a_ap, a_res, evac, prep=None, NW=256, tts=None, side=None, pring=None):
            tts = list(range(TT)) if tts is None else tts
            nsl = ncols_total // NW
            mper = NW // 128
            jobs = [(si, m, tt) for si in range(nsl) for m in range(mper) for tt in tts]
            slot = load_w(w_parts_k(wmat, k0, KC, col0, NW))
            pre_next = prep(0, tts[0]) if prep else None
            for ji, (si, m, tt) in enumerate(jobs):
                mi = si * mper + m
                if m == 0 and tt == tts[0]:
                    cur = slot
                    if si + 1 < nsl:
                        slot = load_w(w_parts_k(wmat, k0, KC, col0 + (si + 1) * NW, NW))
                pre = pre_next
                if prep and ji + 1 < len(jobs):
                    nj = jobs[ji + 1]
                    pre_next = prep(nj[0] * mper + nj[1], nj[2])
                ps = (pring or psring).next()
                wv = cur.ap[:, 0:KC * NW].rearrange("p (k n) -> p k n", n=NW)
                pairs = [(wv[:, k, m * 128:(m + 1) * 128], a_ap(k, tt)) for k in range(KC)]
                ares = list({id(r): r for r in (a_res(k, tt) for k in range(KC))}.values())
                mm_group(ps.ap[:], pairs, [cur.res] + ares, [ps.res])
                evac(mi, tt, ps, pre)
                if side is not None:
                    side()

        def pg_ap(pages):
            return lambda k, tt: arena[:, pages[k], cols(tt)]

        def pg_res(pages):
            return lambda k, tt: R_page[pages[k]]

        def phase_input():
            def ldx(j):
                tt_, c_ = divmod(j, DC)
                s_ = ldring.next()
                src = x_in[tt_ * 512:(tt_ + 1) * 512, c_ * 128:(c_ + 1) * 128].rearrange("(tb p) f -> p tb f", p=128)
                dma("sp", s_.ap[:].rearrange("p (tb f) -> p tb f", f=128), src, [], [s_.res], s_.sem)
                return s_
            PFI = 3
            xs = {j: ldx(j) for j in range(min(PFI, TT * DC))}
            for tt in range(TT):
                for c in range(DC):
                    j = tt * DC + c
                    if j + PFI < TT * DC:
                        xs[j + PFI] = ldx(j + PFI)
                    s = xs.pop(j)
                    ps = psring.next()
                    transpose4(ps, s)
                    o = evring.next()
                    act(o.ap[:], ps.ap[:], AF.Copy, [ps.res], [o.res])
                    store_tile(o, S_xres[c, :, cols(tt)], R_xres[c][tt])
                    P.emit("dve", lambda e, c=c, tt=tt, ps=ps: e.tensor_copy(out=x_ap(c, tt), in_=ps.ap[:]), reads=[ps.res], writes=[x_res(c, tt)])
                    store_xb(c, tt)

        def layer(l):
            last = (l == DEPTH - 1)
            dma("sp", vec_sb[:], vecs[l], [], [R_vec], S_vec)
            dma("pool", gw_sb[:, 0:H, :], rg_wa[l].rearrange("h i j -> i h j"), [], [R_gw], S_gw)
            dma("pool", gw_sb[:, H:2 * H, :], rg_wx[l].rearrange("h i j -> i h j"), [], [R_gw], S_gw)
            act(sct_sb[:], vec_sb[:, V_LAM:V_LAM + H], AF.Exp, [R_vec], [R_sct], scale=-1.0)
            act(sct_sb[:], sct_sb[:], AF.Ln, [R_sct], [R_sct], bias=1.0)
            ts_op(sc_sb[:], sct_sb[:], -C_RG, None, ALU.mult, None, [R_sct], [R_sc])

            HA_pages = list(range(0, HC))
            HB_pages = list(range(HC, DC))

            def load_half(src, rgrid, c0, pages, semi):
                ng = 2
                per = HC // ng
                for g in range(ng):
                    rd = [rgrid[c0 + g * per + i][tt] for i in range(per) for tt in range(TT)]
                    p0 = pages[0] + g * per
                    dma("sp", arena[:, p0:p0 + per, :], src[c0 + g * per:c0 + (g + 1) * per].rearrange("c p t -> p c t"),
                        rd, R_page[p0:p0 + per], asem[semi * 2 + g])

            W = w_in[l]

            def tts_of(sidx):
                return list(range(sidx * TTH, (sidx + 1) * TTH))

            def evac_qk(mi, tt, ps, pre):
                cs, sn = pre
                dst, rs = (S_q, R_q) if mi < H else (S_k, R_k)
                h = mi % H
                sw = evring.next()
                act(sw.ap[0:64, :], ps.ap[64:128, :], AF.Copy, [ps.res], [sw.res])
                act(sw.ap[64:128, :], ps.ap[0:64, :], AF.Copy, [ps.res], [sw.res])
                t1 = evring.next()
                tt_op(t1.ap[:], ps.ap[:], cs.ap[:], ALU.mult, [ps.res, cs.res], [t1.res])
                tt_op(sw.ap[:], sw.ap[:], sn.ap[:], ALU.mult, [sw.res, sn.res], [sw.res])
                ob = bfring.next()
                tt_op(ob.ap[:], t1.ap[:], sw.ap[:], ALU.add, [t1.res, sw.res], [ob.res])
                store_tile(ob, dst[h, :, cols(tt)], rs[h][tt])

            def prep_qk(mi, tt):
                return (load_tile(ldring, cos_t[:, cols(tt)], None), load_tile(ldring, sin_t[:, cols(tt)], None))

            def evac_simple(func, dst, rs, bias_off=None):
                def ev(mi, tt, ps, pre):
                    o = evring.next()
                    act(o.ap[:], ps.ap[:], func, [ps.res] + ([R_vec] if bias_off is not None else []), [o.res],
                        bias=(vcol(bias_off, mi) if bias_off is not None else None))
                    store_tile(o, dst[mi, :, cols(tt)], rs[mi][tt])
                return ev

            def p1a(sidx):
                tts = tts_of(sidx)
                gemm_fm(W, 0, DC, 2 * WA, 0, x_ap, x_res, evac_qk, prep_qk, tts=tts)
                for si in range(WA // 256):
                    slot = load_w(w_parts_k(W, 0, DC, 2 * WA + si * 256, 256))
                    wv = slot.ap[:, 0:DC * 256].rearrange("p (k n) -> p k n", n=256)
                    for tbl in range(TB // 2):
                        tb = sidx * (TB // 2) + tbl
                        ps = psring.next()
                        pairs = [(xh[sidx][:, k, tbl * 128:(tbl + 1) * 128], wv[:, k, :]) for k in range(DC)]
                        mm_group(ps.ap[:, 0:256], pairs, [slot.res] + R_page[sidx * HC:(sidx + 1) * HC], [ps.res])
                        ob = bfring.next()
                        act(ob.ap[:, 0:256], ps.ap[:, 0:256], AF.Copy, [ps.res], [ob.res])
                        store_tile(ob, S_v[tb * 128:(tb + 1) * 128, si * 256:(si + 1) * 256], R_v[tb][si], ap=ob.ap[:, 0:256])
                gemm_fm(W, 0, DC, WA, 3 * WA, x_ap, x_res, evac_simple(AF.Silu, S_g, R_g), tts=tts)
                gemm_fm(W, 0, DC, WA, 4 * WA, x_ap, x_res, evac_simple(AF.Copy, S_xr, R_xr), tts=tts)
                gemm_fm(W, 0, DC, WA, 5 * WA, x_ap, x_res, evac_simple(AF.Gelu_apprx_tanh, S_gr, R_gr), tts=tts)

            def p1b(sidx, side=None):
                tts = tts_of(sidx)
                pr = Ring([psb[6], psb[7]]) if side is not None else None
                gemm_fm(W, 0, DC, D, 6 * WA, x_ap, x_res, evac_simple(AF.Sigmoid, S_sga, R_sga, V_BGA), tts=tts, side=side, pring=pr)
                gemm_fm(W, 0, DC, D, 6 * WA + D, x_ap, x_res, evac_simple(AF.Sigmoid, S_sgb, R_sgb, V_BGB), tts=tts, side=side, pring=pr)

            pg_bytes = T * 2
            work = arena[:, WP0:WP0 + NWP, :].rearrange("p a t -> p (a t)")
            work_res = R_page[WP0:WP0 + NWP]
            nwork = NWP * T

            def work_view(off_b, nbytes, dt):
                e0 = off_b // 2
                ap = work[:, e0:e0 + nbytes // 2]
                if dt == F32:
                    ap = ap.bitcast(F32)
                p0 = off_b // pg_bytes
                p1 = (off_b + nbytes - 1) // pg_bytes
                return ap, work_res[p0:p1 + 1]

            def group_norm(h, it, ops, sg):
                o_sb = evring.next()
                act(o_sb.ap[:], ops.ap[:], AF.Copy, [ops.res], [o_sb.res])
                o2 = evring.next()
                o2h = o2.ap[:].bitcast(BF16)
                act(o2h[:, 0:512], ops.ap[:], AF.Copy, [ops.res], [o2.res])
                act(o2h[:, 512:1024], ops.ap[:], AF.Square, [ops.res], [o2.res])
                mps = psb[6]
                mm_one(mps.ap[:], onesMb[:], o2h[:, 0:512], True, True, [R_onesb, o2.res], [mps.res])
                vps = psb[7]
                mm_one(vps.ap[:], onesMb[:], o2h[:, 512:1024], True, True, [R_onesb, o2.res], [vps.res])
                yield 2
                mu = evring.next()
                act(mu.ap[:], mps.ap[:], AF.Copy, [mps.res], [mu.res])
                tt_op(o2.ap[:], mu.ap[:], mu.ap[:], ALU.mult, [mu.res], [o2.res])
                tt_op(o2.ap[:], vps.ap[:], o2.ap[:], ALU.subtract, [vps.res, o2.res], [o2.res])
                act(o2.ap[:], o2.ap[:], AF.Ln, [o2.res], [o2.res], bias=EPS)
                act(o2.ap[:], o2.ap[:], AF.Exp, [o2.res], [o2.res], scale=-0.5)
                yield 2
                tt_op(o_sb.ap[:], o_sb.ap[:], mu.ap[:], ALU.subtract, [o_sb.res, mu.res], [o_sb.res])
                tt_op(o_sb.ap[:], o_sb.ap[:], o2.ap[:], ALU.mult, [o_sb.res, o2.res], [o_sb.res])
                yb_t = bfring.next()
                stt_op(yb_t.ap[:], o_sb.ap[:], vcol(V_GN, h), sg.ap[:], ALU.mult, ALU.mult,
                       [o_sb.res, sg.res, R_vec], [yb_t.res])
                store_tile(yb_t, S_ya[h, :, cols(it)], R_ya[h][it])
                yield 3

            def side_p2():
                pending = None
                opc = 0
                spc = [0]
                def ld_head(h):
                    par = h % 2
                    base = par * (3 * T * 2 + MW * 4 + 128)
                    base = (base + 63) // 64 * 64
                    qT, rq = work_view(base, T * 2, BF16)
                    kT, rk = work_view(base + T * 2, T * 2, BF16)
                    vv, rv = work_view(base + 2 * T * 2, T * 2, BF16)
                    mk, rm = work_view(base + 3 * T * 2, MW * 4, F32)
                    dma("sp", qT, S_q[h], R_q[h], rq, gring.next())
                    dma("sp", kT, S_k[h], R_k[h], rk, gring.next())
                    vsrc = S_v[:, h * 128:(h + 1) * 128].rearrange("(tb p) e -> p tb e", p=128)
                    dma("sp", vv.rearrange("p (tb e) -> p tb e", e=128), vsrc, [R_v[tb][h // 2] for tb in range(TB)], rv, gring.next())
                    dma("sp", mk, mask_t[h], [], rm, gring.next())
                    return (qT, rq, kT, rk, vv, rv, mk, rm)

                nxt_head = ld_head(0)
                for h in range(H):
                    qT, rq, kT, rk, vv, rv, mk, rm = nxt_head
                    if h + 1 < H:
                        nxt_head = ld_head(h + 1)
                    yield 1
                    for it in range(TT):
                        ops = psb[opc % 2]
                        opc += 1
                        jmax = min(4 * it + 3, TB - 1)
                        sg = load_tile(ldring, S_g[h, :, cols(it)], R_g[h][it])

                        def s_mm(jb, it=it, kT=kT, qT=qT, rq=rq, rk=rk):
                            sps = psb[2 + (spc[0] % 4)]
                            spc[0] += 1
                            mm_one(sps.ap[:], kT[:, jb * 128:(jb + 1) * 128], qT[:, cols(it)], True, True, rq + rk, [sps.res])
                            return sps

                        nxtq = [s_mm(j_) for j_ in range(min(2, jmax + 1))]
                        for jb in range(jmax + 1):
                            sps = nxtq.pop(0)
                            if jb + 2 <= jmax:
                                nxtq.append(s_mm(jb + 2))
                            off = it * 512 - jb * 128 + 384
                            pt = bfring.next()
                            tt_op(pt.ap[:], sps.ap[:], mk[:, off:off + 512], ALU.mult, [sps.res] + rm, [pt.res])
                            mm_one(ops.ap[:], vv[:, jb * 128:(jb + 1) * 128], pt.ap[:], jb == 0, jb == jmax, rv + [pt.res], [ops.res])
                            yield 1
                        if pending is not None:
                            yield from pending
                        pending = group_norm(h, it, ops, sg)
                if pending is not None:
                    yield from pending

            def side_p3():
                TP = T + 4
                p3ring = Ring(psb[2:6])
                nb = [0]

                def wv_(nbytes, dt):
                    ap, rs = work_view(nb[0], nbytes, dt)
                    nb[0] += (nbytes + 63) // 64 * 64
                    return ap, rs
                xrps = [wv_(TP * 4, F32), wv_(TP * 4, F32)]
                xc, r_xc = wv_(T * 4, F32)
                av, r_a = wv_(T * 4, F32)
                ig, r_ig = wv_(T * 4, F32)
                tmp, r_tmp = wv_(T * 4, F32)
                gg, r_gg = wv_(T * 4, F32)
                xcb, r_xcb = wv_(T * 2, BF16)
                hh, r_hh = tmp, r_tmp
                assert nb[0] <= nwork * 2, (nb[0], nwork * 2)

                def ld_xr(c):
                    xrp, r_xrp = xrps[c % 2]
                    P.emit("dve", lambda e, xrp=xrp: e.memset(xrp[:, 0:4], 0.0), writes=r_xrp)
                    dma("sp", xrp[:, 4:4 + T], S_xr[c], R_xr[c], r_xrp, gring.next())

                ld_xr(0)
                for c in range(H):
                    xrp, r_xrp = xrps[c % 2]
                    if c + 1 < H:
                        ld_xr(c + 1)
                    ts_op(xc, xrp[:, 1:1 + T], vcol(V_CW, 0 * H + c), vcol(V_CB, c), ALU.mult, ALU.add, r_xrp + [R_vec], r_xc)
                    yield 3
                    for tap in range(1, 4):
                        stt_op(xc, xrp[:, 1 + tap:1 + tap + T], vcol(V_CW, tap * H + c), xc, ALU.mult, ALU.add, r_xrp + r_xc + [R_vec], r_xc)
                        yield 3
                    act(xcb, xc, AF.Copy, r_xc, r_xcb)
                    dma("sp", gg, S_gr[c], R_gr[c], r_gg, gring.next())
                    yield 14
                    for tt in range(TT):
                        rps = p3ring.next()
                        mm_one(rps.ap[:], gw_sb[:, c, :], xcb[:, cols(tt)], True, True, [R_gw] + r_xcb, [rps.res])
                        ips = p3ring.next()
                        mm_one(ips.ap[:], gw_sb[:, H + c, :], xcb[:, cols(tt)], True, True, [R_gw] + r_xcb, [ips.res])
                        act(av[:, cols(tt)], rps.ap[:], AF.Sigmoid, [rps.res, R_vec], r_a, bias=vcol(V_BA, c))
                        act(ig[:, cols(tt)], ips.ap[:], AF.Sigmoid, [ips.res, R_vec], r_ig, bias=vcol(V_BX, c))
                        yield 1
                    act(av, av, AF.Exp, r_a + [R_sc], r_a, scale=sc_sb[:, c:c + 1])
                    yield 2
                    tt_op(tmp, av, av, ALU.mult, r_a, r_tmp)
                    yield 3
                    ts_op(tmp, tmp, -1.0, 1.0, ALU.mult, ALU.add, r_tmp, r_tmp)
                    yield 2
                    ts_op(tmp, tmp, 1e-30, None, ALU.max, None, r_tmp, r_tmp)
                    yield 2
                    act(tmp, tmp, AF.Sqrt, r_tmp, r_tmp)
                    tt_op(ig, ig, xc, ALU.mult, r_ig + r_xc, r_ig)
                    yield 3
                    tt_op(ig, ig, tmp, ALU.mult, r_ig + r_tmp, r_ig)
                    yield 3
                    P.emit("dve", lambda e, hh=hh, av=av, ig=ig: e.tensor_tensor_scan(out=hh, data0=av, data1=ig, initial=0.0, op0=ALU.mult, op1=ALU.add),
                           reads=r_a + r_ig, writes=r_hh)
                    yield 6
                    tt_op(xcb, hh, gg, ALU.mult, r_hh + r_gg, r_xcb)
                    dma("sp", S_yb[c], xcb, r_xcb, [R_yb[c]], gring.next())
                    yield 3

            def reload_x0():
                ng = 2
                per = DC // ng
                for g in range(ng):
                    pgs = R_page[(g * per) // 2:((g + 1) * per + 1) // 2]
                    dma("sp", xh[0][:, g * per:(g + 1) * per, :], S_xb[g * per:(g + 1) * per].rearrange("c p t -> p c t"),
                        R_xb, pgs, asem[g])

            p1a(0)
            p1a(1)
            for _ in side_p2():
                pass
            gen = side_p3()
            sstate = {"done": False}
            SIDE_BUDGET = 6

            def side():
                b = SIDE_BUDGET
                while b > 0 and not sstate["done"]:
                    try:
                        b -= next(gen)
                    except StopIteration:
                        sstate["done"] = True
                        reload_x0()

            p1b(1, side)
            while not sstate["done"]:
                side()
            dma("sp", arena[:, YAB0:YAB0 + H, :], S_ya.rearrange("h p t -> p h t"),
                [R_ya[h][tt] for h in range(H) for tt in range(TT)], R_page[YAB0:YAB0 + H], asem[2])
            dma("sp", arena[:, YAB0 + H:YAB0 + 2 * H, :], S_yb.rearrange("h p t -> p h t"),
                R_yb, R_page[YAB0 + H:YAB0 + 2 * H], asem[3])
            p1b(0)

            KC4 = 2 * H
            NW4 = 8192 // KC4 if 8192 // KC4 <= 512 else 512
            NW4 = min(NW4, D)
            nsl = D // NW4
            mper = NW4 // 128

            def load_w4(si):
                s = wring.next()
                for half, wm in ((0, w_bra[l]), (1, w_brb[l])):
                    dst = s.ap[:, half * H * NW4:(half + 1) * H * NW4].rearrange("p (k n) -> p k n", n=NW4)
                    dma("pool", dst, wsrc(wm, 0, H, si * NW4, NW4), [], [s.res], s.sem)
                return s

            slot = load_w4(0)
            jobs4 = [(si, m, tt) for si in range(nsl) for m in range(mper) for tt in range(TT)]

            def prep4(j):
                si_, m_, tt_ = jobs4[j]
                mi_ = si_ * mper + m_
                return (load_tile(ldring, S_sga[mi_, :, cols(tt_)], R_sga[mi_][tt_]),
                        load_tile(ldring, S_sgb[mi_, :, cols(tt_)], R_sgb[mi_][tt_]))

            pre4 = prep4(0)
            for si in range(nsl):
                cur = slot
                if si + 1 < nsl:
                    slot = load_w4(si + 1)
                wv = cur.ap[:, 0:KC4 * NW4].rearrange("p (k n) -> p k n", n=NW4)
                for m in range(mper):
                    mi = si * mper + m
                    for tt in range(TT):
                        ji = (si * mper + m) * TT + tt
                        sa, sbt = pre4
                        if ji + 1 < len(jobs4):
                            pre4 = prep4(ji + 1)
                        pa = psring.next()
                        mm_group(pa.ap[:], [(wv[:, k, m * 128:(m + 1) * 128], arena[:, YAB0 + k, cols(tt)]) for k in range(H)],
                                 [cur.res] + R_page[YAB0:YAB0 + H], [pa.res])
                        pb = psring.next()
                        mm_group(pb.ap[:], [(wv[:, H + k, m * 128:(m + 1) * 128], arena[:, YAB0 + H + k, cols(tt)]) for k in range(H)],
                                 [cur.res] + R_page[YAB0 + H:YAB0 + 2 * H], [pb.res])
                        tt_op(sa.ap[:], sa.ap[:], pa.ap[:], ALU.mult, [sa.res, pa.res], [sa.res])
                        tt_op(sbt.ap[:], sbt.ap[:], pb.ap[:], ALU.mult, [sbt.res, pb.res], [sbt.res])
                        if mi < HC:
                            tt_op(arena[:, mi, cols(tt)], sa.ap[:], sbt.ap[:], ALU.add, [sa.res, sbt.res], [R_page[mi]])
                        else:
                            ob = bfring.next()
                            tt_op(ob.ap[:], sa.ap[:], sbt.ap[:], ALU.add, [sa.res, sbt.res], [ob.res])
                            store_tile(ob, S_mg[mi, :, cols(tt)], R_mg[mi][tt])

            def prep_res(mi, tt):
                return load_tile(ldring, S_xres[mi, :, cols(tt)], R_xres[mi][tt])

            def acc_stats(mi, tt, o):
                if mi == 0:
                    P.emit("dve", lambda e: e.tensor_copy(out=accS[tt].ap[:], in_=o.ap[:]), reads=[o.res], writes=[accS[tt].res])
                    act(accQ[tt].ap[:], o.ap[:], AF.Square, [o.res], [accQ[tt].res])
                else:
                    tt_op(accS[tt].ap[:], accS[tt].ap[:], o.ap[:], ALU.add, [accS[tt].res, o.res], [accS[tt].res])
                    sq = evring.next()
                    act(sq.ap[:], o.ap[:], AF.Square, [o.res], [sq.res])
                    tt_op(accQ[tt].ap[:], accQ[tt].ap[:], sq.ap[:], ALU.add, [accQ[tt].res, sq.res], [accQ[tt].res])

            def evac_res_f(with_stats):
                def ev(mi, tt, ps, pre):
                    o = evring.next()
                    stt_op(o.ap[:], pre.ap[:], ALPHA, ps.ap[:], ALU.mult, ALU.add, [pre.res, ps.res], [o.res])
                    if with_stats:
                        acc_stats(mi, tt, o)
                    store_tile(o, S_z[mi, :, cols(tt)], R_z[mi][tt])
                return ev

            def prep_z(mi, tt):
                return load_tile(ldring, S_z[mi, :, cols(tt)], R_z[mi][tt])

            def evac_z_f(with_stats):
                def ev(mi, tt, ps, pre):
                    o = evring.next()
                    tt_op(o.ap[:], pre.ap[:], ps.ap[:], ALU.add, [pre.res, ps.res], [o.res])
                    if with_stats:
                        acc_stats(mi, tt, o)
                    store_tile(o, S_z[mi, :, cols(tt)], R_z[mi][tt])
                return ev

            NWH = 512 if (HC * 512 <= 8192 and D % 512 == 0) else 256
            load_half(S_mg, R_mg, HC, HB_pages, 1)
            gemm_fm(w_o[l], 0, HC, D, 0, pg_ap(HA_pages), pg_res(HA_pages), evac_res_f(False), prep_res, NW=NWH)
            gemm_fm(w_o[l], HC, HC, D, 0, pg_ap(HB_pages), pg_res(HB_pages), evac_z_f(True), prep_z, NW=NWH)

            def layer_norm(goff, boff, final, want_xb=False):
                for tt in range(TT):
                    mps = psring.next()
                    vps = psring.next()
                    mm_one(mps.ap[:], onesD[:], accS[tt].ap[:], True, True, [R_onesD, accS[tt].res], [mps.res])
                    mm_one(vps.ap[:], onesD[:], accQ[tt].ap[:], True, True, [R_onesD, accQ[tt].res], [vps.res])
                    mu = accS[tt]
                    rstd = accQ[tt]
                    act(mu.ap[:], mps.ap[:], AF.Copy, [mps.res], [mu.res])
                    tt_op(rstd.ap[:], mu.ap[:], mu.ap[:], ALU.mult, [mu.res], [rstd.res])
                    tt_op(rstd.ap[:], vps.ap[:], rstd.ap[:], ALU.subtract, [vps.res, rstd.res], [rstd.res])
                    act(rstd.ap[:], rstd.ap[:], AF.Sqrt, [rstd.res], [rstd.res], bias=EPS)
                    P.emit("dve", lambda e, r=rstd: e.reciprocal(out=r.ap[:], in_=r.ap[:]), reads=[rstd.res], writes=[rstd.res])
                for tt in range(TT):
                    mu = accS[tt]
                    rstd = accQ[tt]
                    PF = 3
                    zs = {}
                    for c in range(min(PF, DC)):
                        zs[c] = load_tile(ldring, S_z[c, :, cols(tt)], R_z[c][tt])
                    for c in range(DC):
                        if c + PF < DC:
                            zs[c + PF] = load_tile(ldring, S_z[c + PF, :, cols(tt)], R_z[c + PF][tt])
                        z = zs.pop(c)
                        tt_op(z.ap[:], z.ap[:], mu.ap[:], ALU.subtract, [z.res, mu.res], [z.res])
                        tt_op(z.ap[:], z.ap[:], rstd.ap[:], ALU.mult, [z.res, rstd.res], [z.res])
                        o = evring.next()
                        act(o.ap[:], z.ap[:], AF.Identity, [z.res, R_vec], [o.res], bias=vcol(boff, c), scale=vcol(goff, c))
                        if not final:
                            store_tile(o, S_xres[c, :, cols(tt)], R_xres[c][tt])
                            act(x_ap(c, tt), o.ap[:], AF.Copy, [o.res], [x_res(c, tt)])
                            if want_xb:
                                store_xb(c, tt)
                        else:
                            ps = psring.next()
                            transpose4(ps, o)
                            o2 = evring.next()
                            act(o2.ap[:], ps.ap[:], AF.Copy, [ps.res], [o2.res])
                            dst = y_out[tt * 512:(tt + 1) * 512, c * 128:(c + 1) * 128].rearrange("(tb p) f -> p tb f", p=128)
                            P.emit("sp", lambda e, dst=dst, o2=o2: e.dma_start(out=dst, in_=o2.ap[:].rearrange("p (tb f) -> p tb f", f=128)),
                                   reads=[o2.res], writes=[], dsem=o2.sem)

            layer_norm(V_LN1G, V_LN1B, False)

            def evac_up(mi, tt, ps, pre):
                o = evring.next()
                act(o.ap[:], ps.ap[:], AF.Relu, [ps.res], [o.res])
                ob = bfring.next()
                tt_op(ob.ap[:], o.ap[:], o.ap[:], ALU.mult, [o.res], [ob.res])
                store_tile(ob, S_h[mi, :, cols(tt)], R_h[mi][tt])

            gemm_fm(w_up[l], 0, DC, cfg.DFF, 0, x_ap, x_res, evac_up)

            KB2 = FC // HC
            halves = [HA_pages, HB_pages]
            load_half(S_h, R_h, 0, halves[0], 0)
            for kb2 in range(KB2):
                if kb2 + 1 < KB2:
                    load_half(S_h, R_h, (kb2 + 1) * HC, halves[(kb2 + 1) % 2], (kb2 + 1) % 2)
                lastk = (kb2 == KB2 - 1)
                if kb2 == 0:
                    gemm_fm(w_dn[l], 0, HC, D, 0, pg_ap(halves[0]), pg_res(halves[0]), evac_res_f(lastk), prep_res, NW=NWH)
                else:
                    hp = halves[kb2 % 2]
                    gemm_fm(w_dn[l], kb2 * HC, HC, D, 0, pg_ap(hp), pg_res(hp), evac_z_f(lastk), prep_z, NW=NWH)

            layer_norm(V_LN2G, V_LN2B, last, want_xb=not last)

        phase_input()
        for l in range(DEPTH):
            layer(l)
        P.final_wait("sp", list(P.dma_cnt.keys()))

        esems = {q: es.enter_context(nc.semaphore(f"sem_{q}")) for q in Planner.QS}
        dsems = {k: es.enter_context(nc.semaphore(f"dsem_{k}")) for k in P.dma_cnt}
        block = es.enter_context(nc.Block())
        P.replay(nc, block, esems, dsems)
    return nc


def host_consts(cfg):
    T, H, MW = cfg.T, cfg.H, cfg.MW
    f32 = np.float32
    pos = np.arange(T, dtype=f32)
    inv_freq = (f32(ROPE_BASE) ** (-np.arange(0, 128, 2, dtype=f32) / f32(128))).astype(f32)
    ang = (pos[:, None] * inv_freq[None, :]).astype(f32)
    cos = np.cos(ang).astype(f32).T
    sin = np.sin(ang).astype(f32).T
    cos_t = np.concatenate([cos, cos], 0)
    sin_t = np.concatenate([-sin, sin], 0)
    lg = np.log1p(-np.exp2(-5.0 - np.arange(H, dtype=f32))).astype(f32)
    jl = np.arange(128)[:, None]
    r = np.arange(MW)[None, :] - 384
    ci = np.floor_divide(r, CHUNK)
    cj = jl // CHUNK
    valid = ci >= cj
    dist = np.abs(r - jl).astype(f32)
    mask = np.zeros((H, 128, MW), f32)
    for h in range(H):
        mask[h] = np.where(valid, np.exp(lg[h] * dist) * f32(128 ** -0.5), 0.0)
    return (np.ascontiguousarray(cos_t), np.ascontiguousarray(sin_t), mask, np.eye(128, dtype=f32))


def host_vecs(cfg, ins):
    DC, H, DEPTH = cfg.DC, cfg.H, cfg.DEPTH
    NV = 4 * DC + 2 * DC + 10 * H
    out = np.zeros((DEPTH, 128, NV), np.float32)

    def pc(v, n):
        return np.asarray(v, np.float32).reshape(n, 128).T

    for l in range(DEPTH):
        o = 0
        for name in ("ln1_g", "ln1_b", "ln2_g", "ln2_b"):
            out[l, :, o:o + DC] = pc(ins[name][l], DC); o += DC
        bg = np.asarray(ins["b_gate"][l], np.float32)
        out[l, :, o:o + DC] = pc(bg[:cfg.D], DC); o += DC
        out[l, :, o:o + DC] = pc(bg[cfg.D:], DC); o += DC
        out[l, :, o:o + H] = pc(ins["ret_gn_w"][l], H); o += H
        for tap in range(4):
            out[l, :, o:o + H] = pc(ins["conv_w"][l][tap], H); o += H
        for name in ("conv_b", "rg_ba", "rg_bx", "rg_lambda"):
            out[l, :, o:o + H] = pc(ins[name][l], H); o += H
    return out


_NC_CACHE = {}


def run_cfg(cfg, ins, n_cores, trace=False):
    key = (cfg.D, cfg.H, cfg.DFF, cfg.T, cfg.DEPTH)
    if key not in _NC_CACHE:
        _NC_CACHE[key] = build_program(cfg)
    nc = _NC_CACHE[key]
    cos_t, sin_t, mask, ident = host_consts(cfg)
    vecs = host_vecs(cfg, ins)
    shared = {
        "w_in": np.ascontiguousarray(ins["w_in"], dtype=np.float32),
        "w_br_a": np.ascontiguousarray(ins["w_br_a"], dtype=np.float32),
        "w_br_b": np.ascontiguousarray(ins["w_br_b"], dtype=np.float32),
        "w_o": np.ascontiguousarray(ins["w_o"], dtype=np.float32),
        "w_up": np.ascontiguousarray(ins["w_up"], dtype=np.float32),
        "w_down": np.ascontiguousarray(ins["w_down"], dtype=np.float32),
        "rg_wa": np.ascontiguousarray(ins["rg_wa"], dtype=np.float32),
        "rg_wx": np.ascontiguousarray(ins["rg_wx"], dtype=np.float32),
        "vecs": vecs, "cos_t": cos_t, "sin_t": sin_t, "mask_t": mask, "ident": ident,
    }
    x = np.asarray(ins["x"], dtype=np.float32)
    in_maps = []
    for c in range(n_cores):
        m = dict(shared)
        m["x"] = np.ascontiguousarray(x[c])
        in_maps.append(m)
    res = run_bass_kernel_spmd(nc, in_maps, core_ids=list(range(n_cores)), **({"trace": True} if trace else {}))
    out = np.stack([np.asarray(r["y"], dtype=np.float32) for r in res.results], 0)
    return out, res


def kernel(x, w_in, ret_gn_w, conv_w, conv_b, rg_wa, rg_ba, rg_wx, rg_bx, rg_lambda, w_br_a, w_br_b,
           b_gate, w_o, ln1_g, ln1_b, w_up, w_down, ln2_g, ln2_b):
    cfg = Cfg()
    ins = dict(x=x, w_in=w_in, ret_gn_w=ret_gn_w, conv_w=conv_w, conv_b=conv_b, rg_wa=rg_wa, rg_ba=rg_ba,
               rg_wx=rg_wx, rg_bx=rg_bx, rg_lambda=rg_lambda, w_br_a=w_br_a, w_br_b=w_br_b, b_gate=b_gate,
               w_o=w_o, ln1_g=ln1_g, ln1_b=ln1_b, w_up=w_up, w_down=w_down, ln2_g=ln2_g, ln2_b=ln2_b)
    ins = {k: np.asarray(v) for k, v in ins.items()}
    out, _ = run_cfg(cfg, ins, 8)
    return out.astype(np.float32)
```
